# Optimizing a Trainium2 kernel written in Bass

```python
import math
import jax, jax.numpy as jnp
from jax import lax
import numpy as np

D_MODEL = 1024
BATCH = 32
SEQ = 256
DEPTH = 4
DEC_BATCH = 2
DEC_SEQ = 4096
PAST_LEN = 256

GRID_W = 64
QK_DIM = 64
V_DIM = 2 * QK_DIM
N_HEADS = D_MODEL // (2 * QK_DIM)
FOURIER_GROUPS = 4
FOURIER_WIDTH = D_MODEL // 2
FOURIER_GROUP_DIM = FOURIER_WIDTH // FOURIER_GROUPS
QK_WIDTH = N_HEADS * 2 * QK_DIM
V_WIDTH = N_HEADS * V_DIM
GATE_WIDTH = 2 * D_MODEL
IN_WIDTH = FOURIER_WIDTH + 2 * QK_WIDTH + V_WIDTH + GATE_WIDTH
D_FF = ((8 * D_MODEL // 3 + 127) // 128) * 128
CONV_W = 3
Q_BLOCK = 128
ROPE_BASE = 10000.0
EPS = 1e-6
N_MOD = 6

kernel_name = 'diff_fnet_prefix_dit_step'


def rmsnorm(x, g):
    xf = x.astype(jnp.float32)
    y = xf * lax.rsqrt(jnp.mean(xf * xf, axis=-1, keepdims=True) + EPS)
    return (y * g.astype(jnp.float32)).astype(x.dtype)


def adaln(cond, w, b):
    m = jax.nn.silu(cond) @ w + b
    return [m[:, None, i * D_MODEL:(i + 1) * D_MODEL] for i in range(N_MOD)]


def axial_rope_angles(n_tokens):
    rows = n_tokens // GRID_W
    row = jnp.repeat(jnp.arange(rows), GRID_W).astype(jnp.float32)
    col = jnp.tile(jnp.arange(GRID_W), rows).astype(jnp.float32)
    half = QK_DIM // 2
    inv_freq = 1.0 / (ROPE_BASE ** (jnp.arange(0, half, 2, dtype=jnp.float32) / half))
    return row[:, None] * inv_freq, col[:, None] * inv_freq


def rope_rotate(x, ang):
    n2 = x.shape[-1] // 2
    cos = jnp.cos(ang)[None, :, None, None, :].astype(x.dtype)
    sin = jnp.sin(ang)[None, :, None, None, :].astype(x.dtype)
    x1, x2 = x[..., :n2], x[..., n2:]
    return jnp.concatenate([x1 * cos - x2 * sin, x1 * sin + x2 * cos], axis=-1)


def apply_axial_rope(x, angs):
    row_ang, col_ang = angs
    half = QK_DIM // 2
    return jnp.concatenate([rope_rotate(x[..., :half], row_ang),
                            rope_rotate(x[..., half:], col_ang)], axis=-1)


def diff_attention(q, k, v, lam):
    b, t = q.shape[0], q.shape[1]
    nb = t // Q_BLOCK
    scale = QK_DIM ** -0.5
    qb = jnp.moveaxis(q.reshape(b, nb, Q_BLOCK, N_HEADS, 2, QK_DIM), 1, 0)

    def one_block(qblk):
        s = jnp.einsum('bqhmd,bkhmd->bhmqk', qblk, k).astype(jnp.float32) * scale
        p = jax.nn.softmax(s, axis=-1)
        pd = p[:, :, 0] - lam * p[:, :, 1]
        return jnp.einsum('bhqk,bkhd->bqhd', pd.astype(v.dtype), v)

    o = lax.map(one_block, qb)
    return jnp.moveaxis(o, 0, 1).reshape(b, t, N_HEADS, V_DIM)


def dwconv3(h, w, b):
    hp = jnp.pad(h, ((0, 0), (1, 1), (0, 0)))
    return hp[:, :-2] * w[0] + hp[:, 1:-1] * w[1] + hp[:, 2:] * w[2] + b


def token_mixer(hn, l, P, angs, ctx_k, ctx_v):
    bsz, t, _ = hn.shape
    proj = hn @ P['w_in'][l]
    i0 = FOURIER_WIDTH
    i1 = i0 + QK_WIDTH
    i2 = i1 + QK_WIDTH
    i3 = i2 + V_WIDTH
    f, q, k, v, gates = jnp.split(proj, [i0, i1, i2, i3], axis=-1)
    fg = f.reshape(bsz, t, FOURIER_GROUPS, FOURIER_GROUP_DIM).astype(jnp.float32)
    fr = jnp.real(jnp.fft.fft2(fg, axes=(1, 3), norm='ortho')).astype(hn.dtype)
    a_four = fr.reshape(bsz, t, FOURIER_WIDTH) @ P['w_fourier'][l]
    q = q.reshape(bsz, t, N_HEADS, 2, QK_DIM)
    k = k.reshape(bsz, t, N_HEADS, 2, QK_DIM)
    v = v.reshape(bsz, t, N_HEADS, V_DIM)
    if angs is not None:
        q = apply_axial_rope(q, angs)
        k = apply_axial_rope(k, angs)
    if ctx_k is not None:
        k_all = jnp.concatenate([ctx_k.astype(k.dtype), k], axis=1)
        v_all = jnp.concatenate([ctx_v.astype(v.dtype), v], axis=1)
    else:
        k_all, v_all = k, v
    lam_init = 0.8 - 0.6 * math.exp(-0.3 * l)
    lp = P['lam_params'][l].astype(jnp.float32)
    lam = jnp.exp(jnp.sum(lp[0] * lp[1])) - jnp.exp(jnp.sum(lp[2] * lp[3])) + lam_init
    o = diff_attention(q, k_all, v_all, lam).astype(jnp.float32)
    o = o * lax.rsqrt(jnp.mean(o * o, axis=-1, keepdims=True) + EPS)
    o = (o * P['subln_g'][l].astype(jnp.float32) * (1.0 - lam_init)).astype(hn.dtype)
    a_attn = o.reshape(bsz, t, V_WIDTH) @ P['w_attn'][l]
    g = jax.nn.sigmoid(gates.astype(jnp.float32)).astype(hn.dtype)
    g_four, g_attn = g[..., :D_MODEL], g[..., D_MODEL:]
    out = (g_four * a_four + g_attn * a_attn) @ P['w_o'][l]
    return out, k, v


def channel_mixer(hn, l, P):
    u = hn @ P['w_up'][l]
    u = dwconv3(u, P['conv_w'][l], P['conv_b'][l])
    val, gate = u[..., :D_FF], u[..., D_FF:]
    return (jax.nn.silu(gate) * val) @ P['w_down'][l]


def trunk_layer(x, mods, l, P, angs, ctx_k, ctx_v):
    sh1, sc1, g1, sh2, sc2, g2 = mods
    hn = rmsnorm(x, P['norm1_g'][l]) * (1 + sc1) + sh1
    mix, k, v = token_mixer(hn, l, P, angs, ctx_k, ctx_v)
    x = x + g1 * mix
    hn = rmsnorm(x, P['norm2_g'][l]) * (1 + sc2) + sh2
    x = x + g2 * channel_mixer(hn, l, P)
    return x, k, v


def setup_inputs(seed: int = 0) -> dict:
    key = jax.random.key(seed)
    ks = jax.random.split(key, 24)
    f32 = jnp.float32
    nrm = lambda k, s: jax.random.normal(k, s, dtype=f32)
    D = D_MODEL
    return {
        'x_prompt': nrm(ks[0], (BATCH, SEQ, D)),
        'x_sample': nrm(ks[1], (DEC_BATCH, DEC_SEQ, D)),
        'c': nrm(ks[2], (DEC_BATCH, D)),
        'cache_k': nrm(ks[3], (DEC_BATCH, DEPTH, PAST_LEN, N_HEADS, 2, QK_DIM)),
        'cache_v': nrm(ks[4], (DEC_BATCH, DEPTH, PAST_LEN, N_HEADS, V_DIM)),
        'c_ctx': nrm(ks[5], (D,)),
        'norm1_g': 1.0 + 0.02 * nrm(ks[6], (DEPTH, D)),
        'norm2_g': 1.0 + 0.02 * nrm(ks[7], (DEPTH, D)),
        'final_g': 1.0 + 0.02 * nrm(ks[8], (D,)),
        'w_ada': 0.5 * D ** -0.5 * nrm(ks[9], (DEPTH, D, N_MOD * D)),
        'b_ada': 0.01 * nrm(ks[10], (DEPTH, N_MOD * D)),
        'w_in': D ** -0.5 * nrm(ks[11], (DEPTH, D, IN_WIDTH)),
        'w_fourier': FOURIER_WIDTH ** -0.5 * nrm(ks[12], (DEPTH, FOURIER_WIDTH, D)),
        'lam_params': 0.1 * nrm(ks[13], (DEPTH, 4, QK_DIM)),
        'subln_g': 1.0 + 0.02 * nrm(ks[14], (DEPTH, V_DIM)),
        'w_attn': V_WIDTH ** -0.5 * nrm(ks[15], (DEPTH, V_WIDTH, D)),
        'w_o': D ** -0.5 * nrm(ks[16], (DEPTH, D, D)),
        'w_up': D ** -0.5 * nrm(ks[17], (DEPTH, D, 2 * D_FF)),
        'conv_w': CONV_W ** -0.5 * nrm(ks[18], (DEPTH, CONV_W, 2 * D_FF)),
        'conv_b': 0.01 * nrm(ks[19], (DEPTH, 2 * D_FF)),
        'w_down': D_FF ** -0.5 * nrm(ks[20], (DEPTH, D_FF, D)),
    }


def reference(x_prompt, x_sample, c, cache_k, cache_v, c_ctx, norm1_g, norm2_g, final_g,
              w_ada, b_ada, w_in, w_fourier, lam_params, subln_g, w_attn, w_o,
              w_up, conv_w, conv_b, w_down):
    P = {'norm1_g': norm1_g, 'norm2_g': norm2_g, 'w_in': w_in, 'w_fourier': w_fourier,
         'lam_params': lam_params, 'subln_g': subln_g, 'w_attn': w_attn, 'w_o': w_o,
         'w_up': w_up, 'conv_w': conv_w, 'conv_b': conv_b, 'w_down': w_down}
    h = x_prompt
    ks_list, vs_list = [], []
    for l in range(DEPTH):
        mods = adaln(c_ctx[None, :], w_ada[l], b_ada[l])
        h, k, v = trunk_layer(h, mods, l, P, None, None, None)
        ks_list.append(k)
        vs_list.append(v)
    y_prompt = rmsnorm(h, final_g)
    state_k = jnp.stack(ks_list, axis=1)
    state_v = jnp.stack(vs_list, axis=1)
    angs = axial_rope_angles(x_sample.shape[1])
    z = x_sample
    for l in range(DEPTH):
        mods = adaln(c, w_ada[l], b_ada[l])
        z, _, _ = trunk_layer(z, mods, l, P, angs, cache_k[:, l], cache_v[:, l])
    y_sample = rmsnorm(z, final_g)
    return (y_prompt, y_sample, state_k, state_v)
```

```python
import math
import numpy as np
import ml_dtypes
import concourse.bass as bass
import concourse.mybir as mybir
from concourse.bass_utils import run_bass_kernel_spmd

F32 = mybir.dt.float32
BF16 = mybir.dt.bfloat16
AF = mybir.ActivationFunctionType
ALU = mybir.AluOpType
AX = mybir.AxisListType

D = 1024
DEPTH = 4
DFF = 2816
NJ = 22
INW = 5632
EPS = 1e-6
NTOK = 2048
LAM_INIT = [0.8 - 0.6 * math.exp(-0.3 * l) for l in range(DEPTH)]

O_COND = 0
O_N1G = O_COND + 16
O_N2G = O_N1G + 32
O_FG = O_N2G + 32
O_BADA = O_FG + 8
O_CW = O_BADA + 192
O_CB = O_CW + 528
O_COS = O_CB + 176
O_SIN = O_COS + 1024
O_SEL = O_COS
NPF = O_SEL + 8
NPB = 1024 + 512

DEBUG_STOP = None


import types


def _freeze(fn):
    if fn is None or fn.__closure__ is None:
        return fn
    cells = []
    for c in fn.__closure__:
        try:
            cells.append(types.CellType(c.cell_contents))
        except ValueError:
            cells.append(c)
    return types.FunctionType(fn.__code__, fn.__globals__, fn.__name__, fn.__defaults__, tuple(cells))


class Prog:
    def __init__(self, nc):
        self.nc = nc
        self.eng = {}
        for name, h in (("pe", nc.tensor), ("act", nc.scalar), ("dve", nc.vector), ("pool", nc.gpsimd), ("sp", nc.sync)):
            self.eng[name] = dict(h=h, sem=None, cnt=0, ops=[], waited={})
        self.lanes = {}
        self.lane_rr = {}
        self.last_w = {}
        self.readers = {}
        self.sems = []

    def new_sem(self, name):
        s = self.nc.alloc_semaphore(name=name)
        self.sems.append(s)
        return s

    def setup(self, n_sp=8, n_pool=8, n_cc=6):
        for name in self.eng:
            self.eng[name]["sem"] = self.new_sem("e_" + name)
        for q, n in (("sp", n_sp), ("pool", n_pool)):
            self.lanes[q] = [dict(sem=self.new_sem(f"l_{q}{i}"), cnt=0, unit=16, id=(q, i)) for i in range(n)]
            self.lane_rr[q] = 0
        self.lanes["cc"] = [dict(sem=self.new_sem(f"l_cc{i}"), cnt=0, unit=1, id=("cc", i)) for i in range(n_cc)]
        self.lane_rr["cc"] = 0

    @staticmethod
    def _norm_keys(keys):
        return tuple(("ps", 7) if (isinstance(k, tuple) and len(k) == 2 and k[0] == "psb") else k for k in keys)

    def _deps(self, reads, writes, me=None):
        deps = {}

        def add(tok):
            if tok is None:
                return
            k, v = tok
            if deps.get(k, 0) < v:
                deps[k] = v

        for r in reads:
            add(self.last_w.get(r))
            if isinstance(r, tuple) and len(r) == 2 and r[0] == "ps":
                for k, v in self.readers.get(r, {}).items():
                    if k != me:
                        add((k, v))
        for w in writes:
            add(self.last_w.get(w))
            for k, v in self.readers.get(w, {}).items():
                add((k, v))
        return deps

    def _commit(self, tok, reads, writes):
        k, v = tok
        for r in reads:
            d = self.readers.setdefault(r, {})
            if d.get(k, 0) < v:
                d[k] = v
        for w in writes:
            self.last_w[w] = tok
            self.readers[w] = {}

    def _waits(self, ename, deps):
        e = self.eng[ename]
        out = []
        items = sorted(deps.items(), key=lambda kv: 0 if kv[0] == ("eng", ename) else 1)
        for k, v in items:
            if k == ("eng", "pe") and ename == "pe":
                continue
            if e["waited"].get(k, 0) >= v:
                continue
            e["waited"][k] = v
            if k[0] == "eng":
                out.append((self.eng[k[1]]["sem"], v))
            else:
                ln = self.lanes[k[1][0]][k[1][1]]
                out.append((ln["sem"], v * ln["unit"]))
        return out

    def op(self, ename, fn, reads=(), writes=()):
        e = self.eng[ename]
        reads, writes = self._norm_keys(reads), self._norm_keys(writes)
        deps = self._deps(reads, writes, me=("eng", ename))
        waits = self._waits(ename, deps)
        e["cnt"] += 1
        tok = (("eng", ename), e["cnt"])
        e["ops"].append((waits, _freeze(fn), (e["sem"], 1)))
        self._commit(tok, reads, writes)

    def _lane_op(self, q, lanekind, fn, reads, writes):
        lanes = self.lanes[lanekind]
        i = self.lane_rr[lanekind]
        self.lane_rr[lanekind] = (i + 1) % len(lanes)
        ln = lanes[i]
        deps = self._deps(reads, writes)
        if ln["cnt"] > 0:
            deps[("lane", ln["id"])] = max(deps.get(("lane", ln["id"]), 0), ln["cnt"])
        waits = self._waits(q, deps)
        ln["cnt"] += 1
        tok = (("lane", ln["id"]), ln["cnt"])
        self.eng[q]["ops"].append((waits, _freeze(fn), (ln["sem"], ln["unit"])))
        self._commit(tok, reads, writes)

    def dma(self, q, out, in_, reads=(), writes=()):
        self._lane_op(q, q, lambda h: h.dma_start(out=out, in_=in_), reads, writes)

    def cc(self, fn, reads=(), writes=()):
        self._lane_op("pool", "cc", fn, reads, writes)

    def barrier(self):
        deps = {}
        for name, e in self.eng.items():
            if e["cnt"] > 0:
                deps[("eng", name)] = e["cnt"]
        for q, ls in self.lanes.items():
            if q == "cc":
                continue
            for ln in ls:
                if ln["cnt"] > 0:
                    deps[("lane", ln["id"])] = ln["cnt"]
        for name in self.eng:
            d = dict(deps)
            w = []
            e = self.eng[name]
            for k, v in d.items():
                if e["waited"].get(k, 0) >= v:
                    continue
                e["waited"][k] = v
                if k[0] == "eng":
                    w.append((self.eng[k[1]]["sem"], v))
                else:
                    ln = self.lanes[k[1][0]][k[1][1]]
                    w.append((ln["sem"], v * ln["unit"]))
            if w:
                e["ops"].append((w, None, None))

    def finish_waits(self, ename, keys):
        deps = self._deps(keys, ())
        waits = self._waits(ename, deps)
        self.eng[ename]["ops"].append((waits, None, None))

    def emit(self):
        nc = self.nc
        with nc.Block() as block:
            for name, deco in (("pe", block.tensor), ("act", block.scalar), ("dve", block.vector),
                               ("pool", block.gpsimd), ("sp", block.sync)):
                ops = self.eng[name]["ops"]

                def body(h, ops=ops):
                    for waits, fn, inc in ops:
                        for sem, val in waits:
                            h.wait_ge(sem, val)
                        if fn is not None:
                            ins = fn(h)
                            ins.then_inc(inc[0], inc[1])

                deco(body)


def build_program(dbg=None):
    nc = bass.Bass("TRN2", target_bir_lowering=False)
    WL = DEPTH if dbg is None else dbg.get("nlayers", DEPTH)
    WL = max(WL, 1)
    _orig_sbuf_tensor = nc.sbuf_tensor
    _uniq = [0]

    def _sbuf_tensor(name, shape, dt):
        _uniq[0] += 1
        return _orig_sbuf_tensor(f"{name}_u{_uniq[0]}", shape, dt)
    P = Prog(nc)
    P.setup()

    def din(name, shape, dt=F32):
        return nc.dram_tensor(name, list(shape), dt, kind="ExternalInput").ap()

    def dout(name, shape, dt=F32):
        return nc.dram_tensor(name, list(shape), dt, kind="ExternalOutput").ap()

    xin = din("xin", [NTOK, D])
    w_ada = din("w_ada", [WL, D, 6 * D])
    w_in = din("w_in", [WL, D, INW])
    w_fourier = din("w_fourier", [WL, 512, D])
    w_attn = din("w_attn", [WL, D, D])
    w_o = din("w_o", [WL, D, D])
    w_up = din("w_up", [WL, D, 2 * DFF])
    w_down = din("w_down", [WL, DFF, D])
    cache_k = din("cache_k", [DEPTH, 256, D])
    cache_v = din("cache_v", [DEPTH, 256, D])
    pfm_d = din("pfm", [128, NPF])
    rope_d = din("rope", [128, 2048])
    pbc_d = din("pbc", [128, NPB])
    identf_d = din("identf", [128, 128])
    identb_d = din("identb", [128, 128], BF16)
    swapp_d = din("swapp", [128, 128], BF16)
    dftc_d = din("dftc", [128, 2, 128], BF16)
    dft256_d = din("dft256", [128, 2, 2, 256], BF16)
    dft4k_d = din("dft4k", [8, 128, 2, 32, 128], BF16)

    y_d = dout("y", [NTOK, D])
    sk_d = dout("sk", [4, DEPTH, 256, D])
    sv_d = dout("sv", [4, DEPTH, 256, D])

    def dint(name, shape, dt=BF16):
        return nc.dram_tensor(name, list(shape), dt).ap()

    ib_k_raw = [[dint(f"ibk{l}_{j}", [1024, 512]) for j in range(2)] for l in range(DEPTH)]
    ob_k_raw = [[dint(f"obk{l}_{j}", [4096, 512]) for j in range(2)] for l in range(DEPTH)]
    ib_k = [[a.rearrange("(f h) n -> f (h n)", h=2) for a in row] for row in ib_k_raw]
    ob_k = [[a.rearrange("(f h) n -> f (h n)", h=2) for a in row] for row in ob_k_raw]
    ib_v = [[dint(f"ibv{l}_{j}", [1024, 512]) for j in range(2)] for l in range(DEPTH)]
    ob_v = [[dint(f"obv{l}_{j}", [4096, 512]) for j in range(2)] for l in range(DEPTH)]
    ib_f = [dint(f"ibf{l}", [1024, 512]) for l in range(DEPTH)]
    ob_f = [dint(f"obf{l}", [4096, 512]) for l in range(DEPTH)]
    ib_h = [dint(f"ibh{l}", [128, 128]) for l in range(DEPTH)]
    ob_h = [dint(f"obh{l}", [512, 128]) for l in range(DEPTH)]
    RG = [[0, 1, 2, 3], [4, 5, 6, 7]]

    from contextlib import ExitStack
    es = ExitStack()

    def sb(name, shape, dt=F32):
        return es.enter_context(_sbuf_tensor(name, list(shape), dt))

    xT = sb("xT", [128, 8, NTOK])
    pfm = sb("pfm_sb", [128, NPF])
    identf = sb("identf_sb", [128, 128])
    identb = sb("identb_sb", [128, 128], BF16)
    swapp = sb("swapp_sb", [128, 128], BF16)
    dftc = sb("dftc_sb", [128, 2, 128], BF16)
    dft256 = sb("dft256_sb", [128, 2, 2, 256], BF16)
    onesb = sb("onesb", [128, 128], BF16)
    mods = sb("mods", [128, DEPTH, 48, 2])
    modA = sb("modA", [128, DEPTH, 2, 2, 8])
    lamv = sb("lamv", [128, DEPTH, 2])
    subg = sb("subg", [128, DEPTH, 128])
    scT = sb("scT", [128, 8, 2], BF16)
    NWB = 3
    wbufs = [sb(f"wbuf{i}", [128, 4096], BF16) for i in range(NWB)]
    psum = es.enter_context(nc.psum_tensor("psum", [128, 8, 512], F32))

    ps_rr = [0]

    def psbank(n=7, base=0):
        i = base + ps_rr[0] % n
        ps_rr[0] += 1
        return i

    def PK(i):
        return ("ps", i)

    wstate = dict(issued=0, specs=[])

    def wspec_add(src_ap, shape, view):
        wstate["specs"].append((src_ap, shape, view))
        return len(wstate["specs"]) - 1

    def w_issue_upto(n):
        while wstate["issued"] < min(n, len(wstate["specs"])):
            i = wstate["issued"]
            src, shape, view = wstate["specs"][i]
            buf = wbufs[i % NWB]
            dst = view(buf)
            P.dma("pool", dst, src, reads=(), writes=(("w", i % NWB),))
            wstate["issued"] += 1

    def w_get(i):
        w_issue_upto(i + NWB - 1 + 1 - 0)
        src, shape, view = wstate["specs"][i]
        return view(wbufs[i % NWB]), ("w", i % NWB)

    def wtile(src_ap, view):
        i = wspec_add(src_ap, None, view)
        w_issue_upto(i + 1)
        return view(wbufs[i % NWB]), ("w", i % NWB)

    class WQ:
        def __init__(self):
            self.plan = []
            self.pos = 0
            self.issued = 0

        def add(self, src_ap, view):
            self.plan.append((src_ap, view))

        def _issue(self, upto):
            while self.issued < min(upto, len(self.plan)):
                src, view = self.plan[self.issued]
                slot = self.issued % NWB
                if isinstance(src, list):
                    for si, (s_ap, s_view) in enumerate(src):
                        P.dma("pool", s_view(wbufs[slot]), s_ap, reads=(), writes=(("w", slot),))
                else:
                    P.dma("pool", view(wbufs[slot]), src, reads=(), writes=(("w", slot),))
                self.issued += 1

        def next(self):
            i = self.pos
            self._issue(i + NWB - 1)
            if self.issued <= i:
                self._issue(i + 1)
            self.pos += 1
            src, view = self.plan[i]
            slot = i % NWB
            return view(wbufs[slot]), ("w", slot)

        def keys(self, slot):
            return (("w", slot, 0), ("w", slot, 1))

        def prefetch(self):
            self._issue(self.pos + NWB - 1)

    WQ_ = WQ()

    def v_k512(buf):
        return buf[:, 0:4096].rearrange("p (k n) -> p k n", k=8)

    def v_k4_512(buf):
        return buf[:, 0:2048].rearrange("p (k n) -> p k n", k=4)

    def v_up(buf):
        return buf[:, 0:4096].rearrange("p (k v n) -> p k v n", k=8, v=2)

    def v_down(buf):
        return buf[:, 0:2816].rearrange("p (k n) -> p k n", k=22)

    def src_k512(w, l, c0, kchunks=8):
        return w[l, :, c0:c0 + 512].rearrange("(k p) n -> p k n", p=128)

    def plan_weights():
        def plan_ada(l):
            for nt in range(12):
                WQ_.add(src_k512(w_ada, l, nt * 512), v_k512)
        plan_ada(0)
        def plan_a(l, g):
            for t in ([0, 1, 2, 3, 4, 5, 6] if g == 0 else [1, 2, 3, 4, 5, 6, 0]):
                WQ_.add(src_k512(w_in, l, t * 512), v_k512)

        def plan_b(l):
            for hh in range(2):
                WQ_.add(src_k512(w_in, l, 3584 + hh * 512), v_k512)
                WQ_.add(w_fourier[l, :, hh * 512:(hh + 1) * 512].rearrange("(k p) n -> p k n", p=128), v_k4_512)
            for hh in range(2):
                WQ_.add(src_k512(w_in, l, 4608 + hh * 512), v_k512)
                WQ_.add(src_k512(w_attn, l, hh * 512), v_k512)
            for hh in range(2):
                WQ_.add(src_k512(w_o, l, hh * 512), v_k512)

        def plan_ffn(l):
            for jj in range(11):
                srcs = []
                for vgi in range(2):
                    s_ap = w_up[l][:, vgi * DFF + jj * 256:vgi * DFF + (jj + 1) * 256].rearrange("(k p) n -> p k n", p=128)
                    srcs.append((s_ap, (lambda buf, vgi=vgi: v_up(buf)[:, :, vgi, :])))
                WQ_.add(srcs, v_up)
            for fo in range(8):
                WQ_.add(w_down[l, :, fo * 128:(fo + 1) * 128].rearrange("(k p) n -> p k n", p=128), v_down)

        for l in range(WL):
            plan_a(l, 0)
            if l > 0:
                plan_ffn(l - 1)
            if l + 1 < WL:
                plan_ada(l + 1)
            plan_b(l)
            plan_ffn(l)
            plan_a(l, 1)
            plan_b(l)
        plan_ffn(WL - 1)

    plan_weights()

    def act(fn, reads, writes):
        P.op("act", fn, reads, writes)

    def dve(fn, reads, writes):
        P.op("dve", fn, reads, writes)

    def pe(fn, reads, writes):
        P.op("pe", fn, reads, writes)

    def mm(out, lhsT, rhs, start, stop, reads, writes, **kw):
        pe(lambda h: h.matmul(out, lhsT=lhsT, rhs=rhs, start=start, stop=stop, **kw), reads, writes)

    P.dma("sp", pfm[:], pfm_d[:, :], writes=("pfm",))
    P.dma("sp", identf[:], identf_d[:, :], writes=("identf",))
    P.dma("sp", identb[:], identb_d[:, :], writes=("identb",))
    P.dma("sp", swapp[:], swapp_d[:, :], writes=("swapp",))
    P.dma("sp", dftc[:], dftc_d[:, :, :], writes=("dftc",))
    P.dma("sp", dft256[:], dft256_d[:, :, :, :], writes=("dft256",))
    dve(lambda h: h.memset(onesb[:], 1.0), (), ("onesb",))

    act(lambda h: h.activation(out=scT[:], in_=pfm[:, O_COND:O_COND + 16].rearrange("p (k c) -> p k c", k=8), func=AF.Silu),
        ("pfm",), ("scT",))

    with _sbuf_tensor("xtok0", [128, D], F32) as xtok0, _sbuf_tensor("xtok1", [128, D], F32) as xtok1:
        xtoks = [xtok0, xtok1]
        for tt in range(16):
            xb = xtoks[tt % 2]
            P.dma("sp", xb[:], xin[tt * 128:(tt + 1) * 128, :], writes=(("xtok", tt % 2),))
            for half in range(2):
                bi = psbank()
                for c4 in range(4):
                    c = half * 4 + c4
                    pe(lambda h, bi=bi, c4=c4, c=c, xb=xb: h.transpose(psum[:, bi, c4 * 128:(c4 + 1) * 128], xb[:, c * 128:(c + 1) * 128], identf[:]),
                       (("xtok", tt % 2), "identf"), (PK(bi),))
                dstv = xT[:, half * 4:(half + 1) * 4, tt * 128:(tt + 1) * 128]
                srcv = psum[:, bi, :].rearrange("p (c t) -> p c t", c=4)
                if (tt + half) % 2 == 0:
                    act(lambda h, dstv=dstv, srcv=srcv: h.activation(out=dstv, in_=srcv, func=AF.Copy), (PK(bi),), (("xT", tt // 4),))
                else:
                    dve(lambda h, dstv=dstv, srcv=srcv: h.tensor_copy(out=dstv, in_=srcv), (PK(bi),), (("xT", tt // 4),))

    P.barrier()

    def compute_mods(l):
        for nt in range(12):
            wt, wk = WQ_.next()
            bi = psbank()
            for c in range(4):
                for k in range(8):
                    mm(psum[:, bi, c * 2:c * 2 + 2], wt[:, k, c * 128:(c + 1) * 128], scT[:, k, :], k == 0, k == 7,
                       (wk, "scT"), (PK(bi),))
            for c in range(4):
                ch = nt * 4 + c
                dve(lambda h, bi=bi, c=c, ch=ch, l=l: h.tensor_scalar(out=mods[:, l, ch, :], in0=psum[:, bi, c * 2:c * 2 + 2],
                                                                      scalar1=pfm[:, O_BADA + l * 48 + ch:O_BADA + l * 48 + ch + 1],
                                                                      scalar2=None, op0=ALU.add),
                    (PK(bi), "pfm"), ("mods",))
        for cond in range(2):
            for which in range(2):
                j = 1 if which == 0 else 4
                ng = O_N1G if which == 0 else O_N2G
                dve(lambda h, l=l, cond=cond, which=which, j=j, ng=ng: h.scalar_tensor_tensor(
                    out=modA[:, l, cond, which, :], in0=mods[:, l, j * 8:(j + 1) * 8, cond], scalar=1.0,
                    in1=pfm[:, ng + l * 8:ng + (l + 1) * 8], op0=ALU.add, op1=ALU.mult), ("mods", "pfm"), ("modA",))

    compute_mods(0)
    with _sbuf_tensor("lamtmp", [128, 64], F32) as lamtmp, _sbuf_tensor("lams", [128, 2], F32) as lams, \
            _sbuf_tensor("pbc_sb", [128, NPB], F32) as pbc:
        P.dma("sp", pbc[:], pbc_d[:, :], writes=("pbc",))
        for l in range(DEPTH):
            for i in range(2):
                a0 = l * 256 + (2 * i) * 64
                dve(lambda h, a0=a0: h.tensor_tensor(out=lamtmp[:], in0=pbc[:, a0:a0 + 64], in1=pbc[:, a0 + 64:a0 + 128], op=ALU.mult),
                    ("pbc",), ("lamtmp",))
                dve(lambda h, i=i: h.reduce_sum(out=lams[:, i:i + 1], in_=lamtmp[:], axis=AX.X), ("lamtmp",), ("lams",))
            act(lambda h: h.activation(out=lams[:], in_=lams[:], func=AF.Exp), ("lams",), ("lams",))
            dve(lambda h, l=l: h.scalar_tensor_tensor(out=lamv[:, l, 0:1], in0=lams[:, 0:1], scalar=LAM_INIT[l], in1=lams[:, 1:2],
                                                      op0=ALU.add, op1=ALU.subtract), ("lams",), ("lamv",))
            dve(lambda h, l=l: h.tensor_scalar(out=lamv[:, l, 1:2], in0=lamv[:, l, 0:1], scalar1=-1.0, scalar2=None, op0=ALU.mult),
                ("lamv",), ("lamv",))
            dve(lambda h, l=l: h.tensor_scalar(out=subg[:, l, :], in0=pbc[:, 1024 + l * 128:1024 + (l + 1) * 128],
                                               scalar1=1.0 - LAM_INIT[l], scalar2=None, op0=ALU.mult), ("pbc",), ("subg",))

    P.barrier()
    def rstd_block(T0, rstd, sq, tag=0):
        blk = ("xT", T0 // 512)
        act(lambda h: h.activation(out=sq[:], in_=xT[:, :, T0:T0 + 512], func=AF.Square), (blk,), (("sq", tag),))
        bi = psbank()
        for c in range(8):
            mm(psum[:, bi, :], onesb[:], sq[:, c, :], c == 0, c == 7, (("sq", tag), "onesb"), (PK(bi),))
        dve(lambda h: h.tensor_scalar(out=rstd[:], in0=psum[:, bi, :], scalar1=1.0 / D, scalar2=EPS, op0=ALU.mult, op1=ALU.add),
            (PK(bi),), (("rstd", tag),))
        act(lambda h: h.activation(out=rstd[:], in_=rstd[:], func=AF.Ln), (("rstd", tag),), (("rstd", tag),))
        act(lambda h: h.activation(out=rstd[:], in_=rstd[:], func=AF.Exp, scale=-0.5), (("rstd", tag),), (("rstd", tag),))

    def norm_mod(grp, l, which, hnT, scr):
        rstds, sqs, tmp2 = scr
        cond = grp["cond"]
        shj = 0 if which == 0 else 3
        for b in range(2):
            rstd_block(grp["tok0"] + b * 512, rstds[b], sqs[b], tag=b)
        for b in range(2):
            T0 = grp["tok0"] + b * 512
            rstd = rstds[b]
            for c in range(8):
                t = tmp2[c % 2]
                dve(lambda h, c=c, t=t: h.tensor_tensor(out=t[:], in0=xT[:, c, T0:T0 + 512], in1=rstd[:], op=ALU.mult),
                    (("xT", T0 // 512), ("rstd", b)), (("ntmp", c % 2),))
                act(lambda h, c=c, t=t: h.activation(out=hnT[:, c, b * 512:(b + 1) * 512], in_=t[:], func=AF.Identity,
                                                     scale=modA[:, l, cond, which, c:c + 1],
                                                     bias=mods[:, l, shj * 8 + c, cond:cond + 1]),
                    (("ntmp", c % 2), "modA", "mods"), (("hnT", b),))

    def lin_fm(wt, wk, nk, actT, actkeys, ncols_chunks, epi, chunk0=0):
        pending = None
        for c in range(ncols_chunks):
            for b in range(2):
                bi = psbank()
                for k in range(nk):
                    mm(psum[:, bi, :], wt[:, k, c * 128:(c + 1) * 128], actT[:, k, b * 512:(b + 1) * 512], k == 0, k == nk - 1,
                       (wk, actkeys(b)), (PK(bi),))
                if pending is not None:
                    pending()
                pending = epi(chunk0 + c, b, bi)
        if pending is not None:
            pending()

    def lin_tm(wt, wk, nk, actT, actkeys, epi):
        for tt in range(8):
            bi = psbank()
            for k in range(nk):
                mm(psum[:, bi, :], actT[:, k, tt * 128:(tt + 1) * 128], wt[:, k, :], k == 0, k == nk - 1,
                   (wk, actkeys(tt // 4)), (PK(bi),))
            epi(tt, bi)

    GS = dict(id=0, tok0=0, cond=0, nseq=1, L=1024, rope=True)
    GP = dict(id=1, tok0=1024, cond=1, nseq=4, L=256, rope=False)

    hnT = sb("hnT", [128, 8, 1024], BF16)
    hk = lambda b: ("hnT", b)
    out_keys = []

    def okey_new(name):
        k = (name, len(out_keys))
        out_keys.append(k)
        return k

    def agather(src, dst, rkeys, wkey):
        P.cc(lambda h: h.collective_compute("AllGather", ALU.bypass, replica_groups=RG, ins=[src.opt()], outs=[dst.opt()]),
             reads=tuple(rkeys), writes=(wkey,))

    def proj_f(grp, l, st):
        wt, wk = WQ_.next()

        def epi_f(tt, bi):
            if grp["rope"]:
                dst = st["stage_b"][tt % 2]
                act(lambda h: h.activation(out=dst[:], in_=psum[:, bi, :], func=AF.Copy), (PK(bi),), (("stgb", tt % 2),))
                P.dma("sp", ib_f[l][tt * 128:(tt + 1) * 128, :], dst[:], reads=(("stgb", tt % 2),), writes=(("ibf", l, tt),))
            else:
                act(lambda h: h.activation(out=st["ftok"][:, tt, :], in_=psum[:, bi, :], func=AF.Copy), (PK(bi),), (("ftok", tt // 2),))
        lin_tm(wt, wk, 8, hnT, hk, epi_f)
        if grp["rope"] and not (dbg or {}).get("nocc"):
            agather(ib_f[l], ob_f[l], [("ibf", l, tt) for tt in range(8)], ("obf", l))

    def rope_epi(st, dstT, dkey):
        cosv = st["rope"][:, 0:1024]
        sinv = st["rope"][:, 1024:2048]

        lvl = (dbg or {}).get("ropelvl", 5)

        def epi(hd, b, bi):
            qb = st["ropeb"][b % 2]
            act(lambda h: h.activation(out=qb[:], in_=psum[:, bi, :], func=AF.Copy), (PK(bi),), (("ropeb", b % 2),))
            def tail():
                b2 = psbank()
                mm(psum[:, b2, :], swapp[:], qb[:], True, True, (("ropeb", b % 2), "swapp"), (PK(b2),))
                t1 = st["ropet"][0]
                t2 = st["ropet"][1]
                dve(lambda h: h.tensor_tensor(out=t1[:], in0=psum[:, bi, :], in1=cosv[:, b * 512:(b + 1) * 512], op=ALU.mult),
                    (PK(bi), "rope"), (("ropet", 0),))
                dve(lambda h: h.tensor_tensor(out=t2[:], in0=psum[:, b2, :], in1=sinv[:, b * 512:(b + 1) * 512], op=ALU.mult),
                    (PK(b2), "rope"), (("ropet", 1),))
                dve(lambda h: h.tensor_tensor(out=dstT[:, hd, b * 512:(b + 1) * 512], in0=t1[:], in1=t2[:], op=ALU.add),
                    (("ropet", 0), ("ropet", 1)), (dkey(hd, b),))
            return tail
        return epi

    def plain_epi(dstT, dkey):
        def epi(hd, b, bi):
            if (dbg or {}).get("noepi"):
                return
            if (dbg or {}).get("epi2stg"):
                dst = stg_dbg[b % 2]
                act(lambda h: h.activation(out=dst[:], in_=psum[:, bi, :], func=AF.Copy), (PK(bi),), (("stgb", b % 2),))
                return
            act(lambda h: h.activation(out=dstT[:, hd, b * 512:(b + 1) * 512], in_=psum[:, bi, :], func=AF.Copy),
                (PK(bi),), (dkey(hd, b),))
        return epi

    stop = None if dbg is None else dbg.get("stop")

    stg_dbg = []

    def proj_qkv(grp, l, st, qT):
        stg_dbg[:] = st["stage_b"]
        qkey = (lambda hd, b: ("qT", hd, b)) if (dbg or {}).get("qkeyu") else (lambda hd, b: ("qT",))
        for t in range(2):
            wt, wk = WQ_.next()
            lin_fm(wt, wk, 8, hnT, hk, 4, rope_epi(st, qT, qkey) if (grp["rope"] and not (dbg or {}).get("norope")) else plain_epi(qT, qkey), chunk0=t * 4)
        if stop == "projq":
            return
        if grp["rope"]:
            kTs = st["kTs"]
            kkey = lambda hd, b: ("kTs", hd // 4)
            for t in range(2):
                wt, wk = WQ_.next()
                lin_fm(wt, wk, 8, hnT, hk, 4, rope_epi(st, kTs, kkey), chunk0=t * 4)
                for hq in range(4):
                    hd = t * 4 + hq
                    P.dma("sp", ib_k[l][t][hq * 128:(hq + 1) * 128, :], kTs[:, hd, :], reads=(("kTs", t),), writes=(("ibk", l, t, hq),))
                agather(ib_k_raw[l][t], ob_k_raw[l][t], [("ibk", l, t, hq) for hq in range(4)], ("obk", l, t))
        else:
            kT = st["kT"]
            for t in range(2):
                wt, wk = WQ_.next()

                def epi_k(tt, bi, t=t):
                    s_, t0 = tt // 2, (tt % 2) * 128
                    stg = st["stage_f"][tt % 2]
                    act(lambda h: h.activation(out=stg[:], in_=psum[:, bi, :], func=AF.Copy), (PK(bi),), (("stgf", tt % 2),))
                    P.dma("sp", sk_d[s_, l, t0:t0 + 128, t * 512:(t + 1) * 512], stg[:], reads=(("stgf", tt % 2),), writes=(okey_new("sk"),))
                    nb_ = len(st["stage_b"])
                    kb = st["stage_b"][tt % nb_]
                    dve(lambda h: h.tensor_copy(out=kb[:], in_=psum[:, bi, :]), (PK(bi),), (("stgb", tt % nb_),))
                    pb = tt % 2
                    psb = psum[:, 7, :].bitcast(BF16)[:, pb * 512:(pb + 1) * 512]
                    for hq in range(4):
                        pe(lambda h, hq=hq: h.transpose(psb[:, hq * 128:(hq + 1) * 128], kb[:, hq * 128:(hq + 1) * 128], identb[:]),
                           (("stgb", tt % nb_), "identb"), (("psb", pb),))
                    dve(lambda h: h.tensor_copy(out=kT[:, t * 4:(t + 1) * 4, tt * 128:(tt + 1) * 128],
                                                in_=psb.rearrange("p (c t) -> p c t", c=4)), (("psb", pb),), (("kT", tt // 2),))
                lin_tm(wt, wk, 8, hnT, hk, epi_k)
        for t in range(2):
            wt, wk = WQ_.next()

            def epi_v(tt, bi, t=t):
                if grp["rope"]:
                    dst = st["stage_b"][tt % 2]
                    act(lambda h: h.activation(out=dst[:], in_=psum[:, bi, :], func=AF.Copy), (PK(bi),), (("stgb", tt % 2),))
                    P.dma("sp", ib_v[l][t][tt * 128:(tt + 1) * 128, :], dst[:], reads=(("stgb", tt % 2),), writes=(("ibv", l, t, tt),))
                else:
                    s_, t0 = tt // 2, (tt % 2) * 128
                    stg = st["stage_f"][tt % 2]
                    act(lambda h: h.activation(out=stg[:], in_=psum[:, bi, :], func=AF.Copy), (PK(bi),), (("stgf", tt % 2),))
                    P.dma("sp", sv_d[s_, l, t0:t0 + 128, t * 512:(t + 1) * 512], stg[:], reads=(("stgf", tt % 2),), writes=(okey_new("sv"),))
                    dve(lambda h: h.tensor_copy(out=st["vaug"][:, tt, t * 4:(t + 1) * 4, 0:128],
                                                in_=psum[:, bi, :].rearrange("p (c d) -> p c d", c=4)), (PK(bi),), (("vaug", tt // 2),))
            lin_tm(wt, wk, 8, hnT, hk, epi_v)
            if grp["rope"]:
                agather(ib_v[l][t], ob_v[l][t], [("ibv", l, t, tt) for tt in range(8)], ("obv", l, t))

    def attn_block(l, qTh, NQ, nkc, kT_of, v_of, kv_keys, st, otok, otile0, okey, h_idx):
        nqt = NQ // 128
        E = st["E"]
        nE = len(E)

        def qk(kc):
            sb_ = 2 * (kc % 2)
            kTa = kT_of(kc)
            mm(psum[:, sb_, 0:NQ], kTa[0:64, :], qTh[0:64, :], True, True, kv_keys + (("qT",),), (PK(sb_),))
            mm(psum[:, sb_ + 1, 0:NQ], kTa[64:128, :], qTh[64:128, :], True, True, kv_keys + (("qT",),), (PK(sb_ + 1),))

        def ex(kc):
            sb_ = 2 * (kc % 2)
            Eb = E[kc % nE]
            act(lambda h, Eb=Eb, sb_=sb_: h.activation(out=Eb[:, :, 0:NQ], in_=psum[:, sb_:sb_ + 2, 0:NQ], func=AF.Exp, scale=0.125),
                (PK(sb_), PK(sb_ + 1)), (("E", kc % nE),))

        def pv(kc):
            Eb = E[kc % nE]
            va = v_of(kc)
            for qt in range(nqt):
                for m in range(2):
                    idx = qt * 2 + m
                    bank = 4 + idx // 3
                    col = (idx % 3) * 129
                    first = (kc == 0) and (idx % 3 == 0)
                    mm(psum[:, bank, col:col + 129], Eb[:, m, qt * 128:(qt + 1) * 128], va, first, kc == nkc - 1,
                       (("E", kc % nE),) + kv_keys, (PK(bank),), skip_group_check=True)

        qk(0)
        for kc in range(nkc):
            if kc + 1 < nkc:
                qk(kc + 1)
            ex(kc)
            pv(kc)
        fin = st["fin"]
        nacc = 2 * nqt
        for bank in range(4, 4 + (nacc + 2) // 3):
            a_lo = 3 * (bank - 4)
            n_in = min(3, nacc - a_lo)
            dve(lambda h, bank=bank, a_lo=a_lo, n_in=n_in: h.tensor_copy(
                out=fin[:, a_lo:a_lo + n_in, :], in_=psum[:, bank, 0:n_in * 129].rearrange("p (a c) -> p a c", c=129)),
                (PK(bank),), ("fin",))
        fin_math(l, st, nqt, otok, otile0, okey, h_idx)

    def fin_math(l, st, nqt, otok, otile0, okey, h_idx):
        rec = st["rec"]
        obuf = st["o32"]
        ssb = st["ssb"]
        fin = st["fin"]
        NT = st["NT"]
        sk_ = st.get("ssbk", "ssb")
        fv = fin[:, 0:2 * nqt, :].rearrange("p (q m) c -> p q m c", m=2)
        o = obuf[:, 0:nqt, :]
        c0 = h_idx * NT + otile0
        r0 = rec[:, 0:nqt]
        r1 = rec[:, 4:4 + nqt]
        dve(lambda h: h.reciprocal(out=r0.unsqueeze(2), in_=fv[:, :, 0, 128:129]), ("fin",), ("rec",))
        dve(lambda h: h.reciprocal(out=r1.unsqueeze(2), in_=fv[:, :, 1, 128:129]), ("fin",), ("rec",))
        dve(lambda h: h.tensor_scalar(out=r1, in0=r1, scalar1=lamv[:, l, 1:2], scalar2=None, op0=ALU.mult), ("rec", "lamv"), ("rec",))
        dve(lambda h: h.tensor_tensor(out=o, in0=fv[:, :, 0, 0:128], in1=r0.unsqueeze(2).broadcast_to([128, nqt, 128]), op=ALU.mult),
            ("fin", "rec"), ("o32",))
        dve(lambda h: h.tensor_tensor(out=fv[:, :, 1, 0:128], in0=fv[:, :, 1, 0:128], in1=r1.unsqueeze(2).broadcast_to([128, nqt, 128]), op=ALU.mult),
            ("fin", "rec"), ("fin",))
        dve(lambda h: h.tensor_tensor(out=o, in0=o, in1=fv[:, :, 1, 0:128], op=ALU.add), ("o32", "fin"), ("o32",))
        dve(lambda h: h.tensor_tensor(out=fv[:, :, 0, 0:128], in0=o, in1=o, op=ALU.mult), ("o32", "fin"), ("fin",))
        dve(lambda h: h.reduce_sum(out=ssb[:, c0:c0 + nqt], in_=fv[:, :, 0, 0:128], axis=AX.X), ("fin",),
            tuple((sk_, c) for c in range(c0, c0 + nqt)))
        if st["NT"] == 2:
            act(lambda h: h.activation(out=otok[:, otile0:otile0 + nqt, h_idx * 128:(h_idx + 1) * 128], in_=o, func=AF.Copy), ("o32",), (okey,))
        else:
            dve(lambda h: h.tensor_copy(out=otok[:, otile0:otile0 + nqt, h_idx * 128:(h_idx + 1) * 128], in_=o), ("o32",), (okey,))

    def attn_prompt(l, st, qT, otok2, ssbs, fin_seq):
        kT, vaug, E = st["kT"], st["vaug"], st["E"]
        blocks = [(s_, hd) for s_ in range(4) for hd in range(8)]

        def accpos(aset, idx):
            if idx < 3:
                return (4 if aset == 0 else 6), idx * 129
            return 5, (0 if aset == 0 else 129)

        def qk(i):
            s_, hd = blocks[i]
            pr = 2 * (i % 2)
            qTh = qT[:, hd, s_ * 256:(s_ + 1) * 256]
            for kc in range(2):
                kTa = kT[:, hd, s_ * 256 + kc * 128:s_ * 256 + (kc + 1) * 128]
                mm(psum[:, pr, kc * 256:(kc + 1) * 256], kTa[0:64, :], qTh[0:64, :], True, True, (("kT", s_), ("qT",)), (PK(pr),))
                mm(psum[:, pr + 1, kc * 256:(kc + 1) * 256], kTa[64:128, :], qTh[64:128, :], True, True, (("kT", s_), ("qT",)), (PK(pr + 1),))

        def ex(i):
            pr = 2 * (i % 2)
            Eb = E[i % 2]
            act(lambda h: h.activation(out=Eb[:], in_=psum[:, pr:pr + 2, :], func=AF.Exp, scale=0.125),
                (PK(pr), PK(pr + 1)), (("E", i % 2),))

        def pv(i):
            s_, hd = blocks[i]
            Eb = E[i % 2]
            for kc in range(2):
                va = vaug[:, s_ * 2 + kc, hd, :]
                for qt in range(2):
                    for m in range(2):
                        idx = qt * 2 + m
                        bank, col = accpos(i % 2, idx)
                        first = (kc == 0) and (idx in (0, 3))
                        mm(psum[:, bank, col:col + 129], Eb[:, m, kc * 256 + qt * 128:kc * 256 + (qt + 1) * 128], va, first, kc == 1,
                           (("E", i % 2), ("vaug", s_)), (PK(bank),), skip_group_check=True)

        def finalize(i):
            s_, hd = blocks[i]
            fin = st["fin"]
            b3, _ = accpos(i % 2, 0)
            _, c5 = accpos(i % 2, 3)
            act(lambda h: h.activation(out=fin[:, 0:3, :], in_=psum[:, b3, 0:387].rearrange("p (a c) -> p a c", c=129), func=AF.Copy), (PK(b3),), ("fin",))
            act(lambda h: h.activation(out=fin[:, 3, :], in_=psum[:, 5, c5:c5 + 129], func=AF.Copy), (PK(5),), ("fin",))
            st["ssb"] = ssbs[s_ % 2]
            st["ssbk"] = "ssb%d" % (s_ % 2)
            fin_math(l, st, 2, otok2[s_ % 2], 0, ("otok", s_ % 2), hd)

        qk(0)
        for i in range(32):
            if i + 1 < 32:
                qk(i + 1)
            ex(i)
            pv(i)
            finalize(i)
            s_, hd = blocks[i]
            if hd == 3 and s_ > 0:
                fin_seq(s_ - 1)
        fin_seq(3)

    def attn_post(l, st, otok, ntiles, okey):
        ssb = st["ssb"]
        sk_ = st.get("ssbk", "ssb")
        srk = sk_ + "_r"
        n = ntiles * 8
        rk = tuple((sk_, c) for c in range(n))
        dve(lambda h: h.tensor_scalar(out=ssb[:, 0:n], in0=ssb[:, 0:n], scalar1=1.0 / 128, scalar2=EPS, op0=ALU.mult, op1=ALU.add), rk, (srk,))
        act(lambda h: h.activation(out=ssb[:, 0:n], in_=ssb[:, 0:n], func=AF.Ln), (srk,), (srk,))
        act(lambda h: h.activation(out=ssb[:, 0:n], in_=ssb[:, 0:n], func=AF.Exp, scale=-0.5), (srk,), (srk,))
        for ti in range(ntiles):
            for hd in range(8):
                col = hd * ntiles + ti
                sl = otok[:, ti, hd * 128:(hd + 1) * 128]
                dve(lambda h, sl=sl, col=col: h.scalar_tensor_tensor(out=sl, in0=sl, scalar=ssb[:, col:col + 1], in1=subg[:, l, :],
                                                                     op0=ALU.mult, op1=ALU.mult), (srk, "subg", okey), (okey,) + rk)

    def otok_to_oT(otok, ntiles, oT, tile0, okey):
        for ti in range(ntiles):
            tt = tile0 + ti
            for half in range(2):
                psb = psum[:, 7, :].bitcast(BF16)[:, half * 512:(half + 1) * 512]
                for c4 in range(4):
                    c = half * 4 + c4
                    pe(lambda h, c4=c4, c=c, psb=psb, ti=ti: h.transpose(psb[:, c4 * 128:(c4 + 1) * 128], otok[:, ti, c * 128:(c + 1) * 128], identb[:]),
                       (okey, "identb"), (("psb", half),))
                dstv = oT[:, half * 4:(half + 1) * 4, tt * 128:(tt + 1) * 128]
                srcv = psb.rearrange("p (c t) -> p c t", c=4)
                if half == 0:
                    act(lambda h, dstv=dstv, srcv=srcv: h.activation(out=dstv, in_=srcv, func=AF.Copy), (("psb", half),), (("oT", tt // 4),))
                else:
                    dve(lambda h, dstv=dstv, srcv=srcv: h.tensor_copy(out=dstv, in_=srcv), (("psb", half),), (("oT", tt // 4),))

    def fourier_finish(AB, abkey, ABT, tt):
        for ab in range(2):
            psb = psum[:, 7, :].bitcast(BF16)[:, ab * 512:(ab + 1) * 512]
            for g in range(4):
                pe(lambda h, g=g, psb=psb, ab=ab: h.transpose(psb[:, g * 128:(g + 1) * 128], AB[:, ab, g * 128:(g + 1) * 128], identb[:]),
                   (abkey, "identb"), (("psb", ab),))
            t4 = tt % 4
            dstv = ABT[:, :, ab, t4 * 128:(t4 + 1) * 128]
            srcv = psb.rearrange("p (c t) -> p c t", c=4)
            if ab == 0:
                act(lambda h, dstv=dstv, srcv=srcv: h.activation(out=dstv, in_=srcv, func=AF.Copy), (("psb", ab),), ("ABT",))
            else:
                dve(lambda h, dstv=dstv, srcv=srcv: h.tensor_copy(out=dstv, in_=srcv), (("psb", ab),), ("ABT",))

    def chan_dft(ABT, frT, b):
        for g in range(4):
            bi = psbank(4)
            mm(psum[:, bi, :], dftc[:, 0, :], ABT[:, g, 0, :], True, False, ("dftc", "ABT"), (PK(bi),))
            mm(psum[:, bi, :], dftc[:, 1, :], ABT[:, g, 1, :], False, True, ("dftc", "ABT"), (PK(bi),))
            act(lambda h, g=g, b=b, bi=bi: h.activation(out=frT[:, g, b * 512:(b + 1) * 512], in_=psum[:, bi, :], func=AF.Copy),
                (PK(bi),), (("frT", b),))

    def stage_mix_out(grp, l, oT, frT, gsT, st):
        cond = grp["cond"]
        for pas in range(2):
            srcT, nk2, skey = (frT, 4, (lambda b: ("frT", b))) if pas == 0 else (oT, 8, (lambda b: ("oT", b)))
            for hh in range(2):
                wg, wgk = WQ_.next()
                wa, wak = WQ_.next()
                for c in range(4):
                    fo = hh * 4 + c
                    for b in range(2):
                        bg = psbank()
                        for k in range(8):
                            mm(psum[:, bg, :], wg[:, k, c * 128:(c + 1) * 128], hnT[:, k, b * 512:(b + 1) * 512], k == 0, k == 7,
                               (wgk, hk(b)), (PK(bg),))
                        ba = psbank()
                        for k in range(nk2):
                            mm(psum[:, ba, :], wa[:, k, c * 128:(c + 1) * 128], srcT[:, k, b * 512:(b + 1) * 512], k == 0, k == nk2 - 1,
                               (wak, skey(b)), (PK(ba),))
                        sg = st["sg"][(fo * 2 + b) % 2]
                        sgk = ("sg", (fo * 2 + b) % 2)
                        act(lambda h, sg=sg, bg=bg: h.activation(out=sg[:], in_=psum[:, bg, :], func=AF.Sigmoid), (PK(bg),), (sgk,))
                        gdst = gsT[:, fo, b * 512:(b + 1) * 512]
                        if pas == 0:
                            dve(lambda h, sg=sg, ba=ba, gdst=gdst: h.tensor_tensor(out=gdst, in0=sg[:], in1=psum[:, ba, :], op=ALU.mult),
                                (sgk, PK(ba)), (("gsT", b),))
                        else:
                            dve(lambda h, sg=sg, ba=ba: h.tensor_tensor(out=sg[:], in0=sg[:], in1=psum[:, ba, :], op=ALU.mult),
                                (sgk, PK(ba)), (sgk,))
                            dve(lambda h, sg=sg, gdst=gdst: h.tensor_tensor(out=gdst, in0=gdst, in1=sg[:], op=ALU.add),
                                (sgk, ("gsT", b)), (("gsT", b),))
        for hh in range(2):
            wt, wk = WQ_.next()

            def epi_o(fo, b, bi):
                T0 = grp["tok0"] + b * 512
                dve(lambda h: h.scalar_tensor_tensor(out=xT[:, fo, T0:T0 + 512], in0=psum[:, bi, :],
                                                     scalar=mods[:, l, 2 * 8 + fo, cond:cond + 1], in1=xT[:, fo, T0:T0 + 512],
                                                     op0=ALU.mult, op1=ALU.add), (PK(bi), "mods", ("xT", T0 // 512)), (("xT", T0 // 512),))
            lin_fm(wt, wk, 8, gsT, lambda b: ("gsT", b), 4, epi_o, chunk0=hh * 4)

    def stage_ffn(grp, l, st):
        cond = grp["cond"]
        nseq, L = grp["nseq"], grp["L"]
        hT = st["hT"]
        halo = None
        if grp["rope"]:
            hst = st["hst"]
            dve(lambda h: h.tensor_copy(out=hst[:, :, 0:1], in_=hnT[:, :, 0:1]), (hk(0),), ("hst",))
            dve(lambda h: h.tensor_copy(out=hst[:, :, 1:2], in_=hnT[:, :, 1023:1024]), (hk(1),), ("hst",))
            P.dma("sp", ib_h[l][:, 0:16], hst[:].rearrange("p k w -> p (k w)"), reads=("hst",), writes=(("ibh", l),))
            agather(ib_h[l], ob_h[l], [("ibh", l)], ("obh", l))
            hall = st["hall"]
            P.dma("sp", hall[:], ob_h[l].rearrange("(r p) n -> p r n", p=128)[:, :, 0:16], reads=(("obh", l),), writes=("hall",))
            hsel32 = st["hsel32"]
            hv = hall[:].rearrange("p r (k w) -> p r k w", w=2)
            for w_, so in ((0, 0), (1, 4)):
                srcw = 1 - w_
                for r in range(4):
                    sc = pfm[:, O_SEL + so + r:O_SEL + so + r + 1]
                    if r == 0:
                        dve(lambda h, sc=sc, r=r, srcw=srcw, w_=w_: h.tensor_scalar(out=hsel32[:, :, w_], in0=hv[:, r, :, srcw], scalar1=sc, scalar2=None, op0=ALU.mult),
                            ("hall", "pfm"), ("hsel32",))
                    else:
                        dve(lambda h, sc=sc, r=r, srcw=srcw, w_=w_: h.scalar_tensor_tensor(out=hsel32[:, :, w_], in0=hv[:, r, :, srcw], scalar=sc, in1=hsel32[:, :, w_],
                                                                                          op0=ALU.mult, op1=ALU.add), ("hall", "pfm", "hsel32"), ("hsel32",))
            halo = st["hsel"]
            dve(lambda h: h.tensor_copy(out=halo[:], in_=hsel32[:]), ("hsel32",), ("hsel",))
        slot = 0
        for jj in range(11):
            wt, wk = WQ_.next()
            for j2 in range(2):
                j = jj * 2 + j2
                cbuf = []
                for vg in range(2):
                    pb0 = 2 * (slot % 3)
                    slot += 1
                    f = vg * 22 + j
                    wofs = O_CW + l * 132
                    w0 = pfm[:, wofs + f:wofs + f + 1]
                    w1 = pfm[:, wofs + 44 + f:wofs + 44 + f + 1]
                    w2 = pfm[:, wofs + 88 + f:wofs + 88 + f + 1]
                    bb = pfm[:, O_CB + l * 44 + f:O_CB + l * 44 + f + 1]
                    wsl = lambda k, vg=vg, j2=j2: wt[:, k, vg, j2 * 128:(j2 + 1) * 128]
                    for b in range(2):
                        for k in range(8):
                            mm(psum[:, pb0 + b, :], wsl(k), hnT[:, k, b * 512:(b + 1) * 512], k == 0, k == 7,
                               (wk, hk(b)), (PK(pb0 + b),))
                    hb = 6 + (slot % 2)
                    if halo is not None:
                        for k in range(8):
                            mm(psum[:, hb, 0:2], wsl(k), halo[:, k, :], k == 0, k == 7, (wk, "hsel"), (PK(hb),))
                    cb = st["cbuf"][vg]
                    ckey = ("cbuf", vg)
                    pk = (PK(pb0), PK(pb0 + 1))
                    pfull = psum[:, pb0:pb0 + 2, :].rearrange("p a n -> p (a n)")
                    act(lambda h, cb=cb, pfull=pfull, w1=w1, bb=bb: h.activation(out=cb[:], in_=pfull, func=AF.Identity, scale=w1, bias=bb),
                        pk + ("pfm",), (ckey,))
                    cv = cb[:].rearrange("p (s t) -> p s t", s=nseq)
                    pv = pfull.rearrange("p (s t) -> p s t", s=nseq)
                    dve(lambda h, cv=cv, pv=pv, w0=w0: h.scalar_tensor_tensor(out=cv[:, :, 1:L], in0=pv[:, :, 0:L - 1], scalar=w0, in1=cv[:, :, 1:L],
                                                                             op0=ALU.mult, op1=ALU.add), pk + ("pfm", ckey), (ckey,))
                    dve(lambda h, cv=cv, pv=pv, w2=w2: h.scalar_tensor_tensor(out=cv[:, :, 0:L - 1], in0=pv[:, :, 1:L], scalar=w2, in1=cv[:, :, 0:L - 1],
                                                                             op0=ALU.mult, op1=ALU.add), pk + ("pfm", ckey), (ckey,))
                    if halo is not None:
                        dve(lambda h, cb=cb, hb=hb, w0=w0: h.scalar_tensor_tensor(out=cb[:, 0:1], in0=psum[:, hb, 0:1], scalar=w0, in1=cb[:, 0:1],
                                                                                 op0=ALU.mult, op1=ALU.add), (PK(hb), "pfm", ckey), (ckey,))
                        dve(lambda h, cb=cb, hb=hb, w2=w2: h.scalar_tensor_tensor(out=cb[:, 1023:1024], in0=psum[:, hb, 1:2], scalar=w2, in1=cb[:, 1023:1024],
                                                                                 op0=ALU.mult, op1=ALU.add), (PK(hb), "pfm", ckey), (ckey,))
                    cbuf.append((cb, ckey))
                cbv, cvk = cbuf[0]
                cbg, cgk = cbuf[1]
                act(lambda h, cbg=cbg: h.activation(out=cbg[:], in_=cbg[:], func=AF.Silu), (cgk,), (cgk,))
                dve(lambda h, cbv=cbv, cbg=cbg, j=j: h.tensor_tensor(out=hT[:, j, :], in0=cbg[:], in1=cbv[:], op=ALU.mult), (cvk, cgk), (("hT", j),))
        for fo in range(8):
            wt, wk = WQ_.next()
            for b in range(2):
                bi = psbank()
                for k in range(NJ):
                    mm(psum[:, bi, :], wt[:, k, :], hT[:, k, b * 512:(b + 1) * 512], k == 0, k == NJ - 1, (wk, ("hT", k)), (PK(bi),))
                T0 = grp["tok0"] + b * 512
                dve(lambda h, fo=fo, bi=bi, T0=T0: h.scalar_tensor_tensor(out=xT[:, fo, T0:T0 + 512], in0=psum[:, bi, :],
                                                                         scalar=mods[:, l, 5 * 8 + fo, cond:cond + 1], in1=xT[:, fo, T0:T0 + 512],
                                                                         op0=ALU.mult, op1=ALU.add), (PK(bi), "mods", ("xT", T0 // 512)), (("xT", T0 // 512),))

    def scope():
        ph = ExitStack()
        ph.callback(P.barrier)

        def alloc(name, shape, dt=F32):
            return ph.enter_context(_sbuf_tensor(name, list(shape), dt))
        return ph, alloc

    def do_norm(grp, l, which):
        ph, a = scope()
        with ph:
            scr = ([a("rstd0", [128, 512]), a("rstd1", [128, 512])], [a("sq0", [128, 8, 512], BF16), a("sq1", [128, 8, 512], BF16)],
                   [a("ntmp0", [128, 512]), a("ntmp1", [128, 512])])
            norm_mod(grp, l, which, hnT, scr)

    def attn_small(a, st, nq=512, ecols=None):
        st["E"] = [a(f"E{i}", [128, 2, ecols or nq], BF16) for i in range(2)]
        st["rec"] = a("rec", [128, 8])
        st["o32"] = a("o32", [128, nq // 128, 128])
        st["NT"] = 8 if nq == 512 else 2
        st["fin"] = a("finbuf", [128, 2 * (nq // 128), 129])
        if "ssb" not in st:
            st["ssb"] = a("ssb", [128, 64 if nq == 512 else 16])

    nlayers = DEPTH if dbg is None else dbg.get("nlayers", DEPTH)
    stop = None if dbg is None else dbg.get("stop")

    def tokmix(grp, l, hook=None):
        ph0, a0 = scope()
        with ph0:
            frT = a0("frT", [128, 4, 1024], BF16)
            if grp["rope"]:
                qT = a0("qT", [128, 8, 1024], BF16)
            do_norm(grp, l, 0)
            if grp["rope"]:
                ph1, a1 = scope()
                with ph1:
                    st = dict()
                    st["stage_b"] = [a1("stgb0", [128, 512], BF16), a1("stgb1", [128, 512], BF16)]
                    st["ropeb"] = [a1("ropeb0", [128, 512], BF16), a1("ropeb1", [128, 512], BF16)]
                    st["ropet"] = [a1("ropet0", [128, 512]), a1("ropet1", [128, 512])]
                    st["kTs"] = a1("kTs", [128, 8, 1024], BF16)
                    st["rope"] = a1("rope_sb", [128, 2048])
                    P.dma("sp", st["rope"][:], rope_d[:, :], writes=("rope",))
                    if not (dbg or {}).get("nof"):
                        proj_f(grp, l, st)
                    if (dbg or {}).get("extraw"):
                        WQ_.next()
                        WQ_.next()
                        WQ_.next()
                    if stop != "projf":
                        proj_qkv(grp, l, st, qT)
                if stop in ("projf", "proj", "projq"):
                    return
                if hook is not None:
                    hook()
                ph1, a1 = scope()
                with ph1:
                    fbuf = a1("fbuf", [128, 32, 512], BF16)
                    cs = a1("cs0", [128, 2, 32, 128], BF16)
                    AB = [a1("AB0", [128, 2, 512], BF16)]
                    ABT = a1("ABT", [128, 4, 2, 512], BF16)
                    for q4 in range(4):
                        P.dma("sp", fbuf[:, q4 * 8:(q4 + 1) * 8, :], ob_f[l].rearrange("(c p) n -> p c n", p=128)[:, q4 * 8:(q4 + 1) * 8, :],
                              reads=(("obf", l),), writes=(("fbuf", q4),))
                    for tt in range(8):
                        P.dma("sp", cs[:], dft4k_d[tt], writes=("cs",))
                        ba, bb_ = psbank(4), psbank(4)
                        for ci, bnk in ((0, ba), (1, bb_)):
                            for kc in range(32):
                                mm(psum[:, bnk, :], cs[:, ci, kc, :], fbuf[:, kc, :], kc == 0, kc == 31, ("cs", ("fbuf", kc // 8)), (PK(bnk),))
                        ab = AB[0]
                        act(lambda h, ab=ab, ba=ba: h.activation(out=ab[:, 0, :], in_=psum[:, ba, :], func=AF.Copy), (PK(ba),), ("AB",))
                        dve(lambda h, ab=ab, bb_=bb_: h.tensor_copy(out=ab[:, 1, :], in_=psum[:, bb_, :]), (PK(bb_),), ("AB",))
                        fourier_finish(ab, "AB", ABT, tt)
                        if tt % 4 == 3:
                            chan_dft(ABT, frT, tt // 4)
                if stop == "fourier":
                    return
                ph1, a1 = scope()
                with ph1:
                    otok = a1("otok", [128, 8, 1024], BF16)
                    st = dict()
                    st["ssb"] = a1("ssb", [128, 64])
                    ph2, a2 = scope()
                    with ph2:
                        attn_small(a2, st)
                        kbs = [a2("kbuf0", [128, 4352], BF16), a2("kbuf1", [128, 4352], BF16)]
                        vbs = [a2("vbuf0", [128, 34, 129], BF16), a2("vbuf1", [128, 34, 129], BF16)]
                        kstg = a2("kstg", [128, 2, 1024], BF16)
                        for i_ in range(2):
                            dve(lambda h, i_=i_: h.memset(vbs[i_][:, :, 128:129], 1.0), (), (("vbuf", i_, "o"),))
                        P.dma("pool", kstg[:], cache_k[l].rearrange("(c p) n -> p c n", p=128), writes=("kstg",))

                        def load_kv(hd):
                            kb, vb, pi = kbs[hd % 2], vbs[hd % 2], hd % 2
                            psb = psum[:, 7, :].bitcast(BF16)[:, 0:256]
                            for c in range(2):
                                pe(lambda h, c=c, hd=hd, psb=psb: h.transpose(psb[:, c * 128:(c + 1) * 128], kstg[:, c, hd * 128:(hd + 1) * 128], identb[:]),
                                   ("kstg", "identb"), (("psb", 0),))
                            dve(lambda h, psb=psb, kb=kb: h.tensor_copy(out=kb[:, 0:256], in_=psb), (("psb", 0),), (("kbuf", pi, "c"),))
                            t, hq = hd // 4, hd % 4
                            P.dma("sp", kb[:, 256:4352].rearrange("p (r n) -> p r n", r=4),
                                  ob_k[l][t].rearrange("(r f) n -> f r n", r=4)[hq * 128:(hq + 1) * 128, :, :],
                                  reads=(("obk", l, t),), writes=(("kbuf", pi, "g"),))
                            P.dma("pool", vb[:, 0:2, 0:128], cache_v[l].rearrange("(c p) n -> p c n", p=128)[:, :, hd * 128:(hd + 1) * 128],
                                  writes=(("vbuf", pi, "c"),))
                            P.dma("sp", vb[:, 2:34, 0:128], ob_v[l][t].rearrange("(c p) n -> p c n", p=128)[:, :, hq * 128:(hq + 1) * 128],
                                  reads=(("obv", l, t),), writes=(("vbuf", pi, "g"),))

                        load_kv(0)
                        for hd in range(8):
                            if hd + 1 < 8:
                                load_kv(hd + 1)
                            kb, vb, pi = kbs[hd % 2], vbs[hd % 2], hd % 2
                            kvk = (("kbuf", pi, "c"), ("kbuf", pi, "g"), ("vbuf", pi, "c"), ("vbuf", pi, "g"), ("vbuf", pi, "o"))
                            for qb in range(2):
                                attn_block(l, qT[:, hd, qb * 512:(qb + 1) * 512], 512, 34,
                                           lambda kc, kb=kb: kb[:, kc * 128:(kc + 1) * 128],
                                           lambda kc, vb=vb: vb[:, kc, :],
                                           kvk, st, otok, qb * 4, ("otok", 0), hd)
                    oT = a1("oT", [128, 8, 1024], BF16)
                    attn_post(l, st, otok, 8, ("otok", 0))
                    otok_to_oT(otok, 8, oT, 0, ("otok", 0))
                    if stop == "attn":
                        return
                    ph2, a2 = scope()
                    with ph2:
                        gsT = a2("gsT", [128, 8, 1024], BF16)
                        st = dict(sg=[a2("sg0", [128, 512]), a2("sg1", [128, 512])])
                        stage_mix_out(grp, l, oT, frT, gsT, st)
            else:
                ph1, a1 = scope()
                with ph1:
                    oT = a1("oT", [128, 8, 1024], BF16)
                    ph2, a2 = scope()
                    with ph2:
                        st = dict()
                        qT = a2("qT", [128, 8, 1024], BF16)
                        st["kT"] = a2("kT", [128, 8, 1024], BF16)
                        st["vaug"] = a2("vaug", [128, 8, 8, 129], BF16)
                        otok2 = [a2("otok0", [128, 2, 1024], BF16), a2("otok1", [128, 2, 1024], BF16)]
                        vaug = st["vaug"]
                        dve(lambda h: h.memset(vaug[:, :, :, 128:129], 1.0), (), (("vaug", 0), ("vaug", 1), ("vaug", 2), ("vaug", 3)))
                        ph3, a3 = scope()
                        with ph3:
                            st["stage_f"] = [a3("stgf0", [128, 512]), a3("stgf1", [128, 512])]
                            st["stage_b"] = [a3("stgb0", [128, 512], BF16), a3("stgb1", [128, 512], BF16)]
                            proj_qkv(grp, l, st, qT)
                        attn_small(a2, st, nq=256, ecols=512)
                        ssbs = [st["ssb"], a2("ssb_b", [128, 16])]

                        def fin(sq):
                            st["ssb"] = ssbs[sq % 2]
                            st["ssbk"] = "ssb%d" % (sq % 2)
                            attn_post(l, st, otok2[sq % 2], 2, ("otok", sq % 2))
                            otok_to_oT(otok2[sq % 2], 2, oT, sq * 2, ("otok", sq % 2))
                        attn_prompt(l, st, qT, otok2, ssbs, fin)
                    ph2, a2 = scope()
                    with ph2:
                        ph3, a3 = scope()
                        with ph3:
                            st = dict()
                            st["ftok"] = a3("ftok", [128, 8, 512], BF16)
                            AB = [a3("AB0", [128, 2, 512], BF16), a3("AB1", [128, 2, 512], BF16)]
                            ABT = a3("ABT", [128, 4, 2, 512], BF16)
                            proj_f(grp, l, st)
                            for s_ in range(4):
                                for jt in range(2):
                                    tt = s_ * 2 + jt
                                    ba, bb_ = psbank(4), psbank(4)
                                    for csi, bnk in ((0, ba), (1, bb_)):
                                        for kc in range(2):
                                            mm(psum[:, bnk, :], dft256[:, csi, kc, jt * 128:(jt + 1) * 128], st["ftok"][:, s_ * 2 + kc, :], kc == 0, kc == 1,
                                               ("dft256", ("ftok", s_)), (PK(bnk),))
                                    ab = AB[tt % 2]
                                    act(lambda h, ab=ab, ba=ba: h.activation(out=ab[:, 0, :], in_=psum[:, ba, :], func=AF.Copy), (PK(ba),), (("AB", tt % 2),))
                                    dve(lambda h, ab=ab, bb_=bb_: h.tensor_copy(out=ab[:, 1, :], in_=psum[:, bb_, :]), (PK(bb_),), (("AB", tt % 2),))
                                    fourier_finish(ab, ("AB", tt % 2), ABT, tt)
                                    if tt % 4 == 3:
                                        chan_dft(ABT, frT, tt // 4)
                        gsT = a2("gsT", [128, 8, 1024], BF16)
                        st = dict(sg=[a2("sg0", [128, 512]), a2("sg1", [128, 512])])
                        stage_mix_out(grp, l, oT, frT, gsT, st)

    def ffn(grp, l):
        do_norm(grp, l, 1)
        ph0, a0 = scope()
        with ph0:
            st = dict()
            st["hT"] = a0("hT", [128, NJ, 1024], BF16)
            st["cbuf"] = [a0("cbufv", [128, 1024]), a0("cbufg", [128, 1024])]
            st["uh"] = [a0("uh0", [128, 2]), a0("uh1", [128, 2])]
            if grp["rope"]:
                st["hst"] = a0("hst", [128, 8, 2], BF16)
                st["hall"] = a0("hall", [128, 4, 16], BF16)
                st["hsel32"] = a0("hsel32", [128, 8, 2])
                st["hsel"] = a0("hsel", [128, 8, 2], BF16)
            stage_ffn(grp, l, st)


    for l in range(nlayers):
        def hook(l=l):
            if l > 0:
                ffn(GP, l - 1)
            if l + 1 < nlayers:
                compute_mods(l + 1)
            if l > 0:
                do_norm(GS, l, 0)
        tokmix(GS, l, hook)
        if stop is not None:
            continue
        ffn(GS, l)
        tokmix(GP, l)
    if stop is None and nlayers > 0:
        ffn(GP, nlayers - 1)

    P.barrier()
    with ExitStack() as ph:
        def p2(name, shape, dt=F32, ph=ph):
            return ph.enter_context(_sbuf_tensor(name, list(shape), dt))
        rstd = p2("rstd", [128, 512])
        sq = p2("sq", [128, 8, 512], BF16)
        ynT = p2("ynT", [128, 8, 512])
        ytok = [p2("ytok0", [128, D]), p2("ytok1", [128, D])]
        for blk in range(4):
            T0 = blk * 512
            rstd_block(T0, rstd, sq)
            for c in range(8):
                dve(lambda h, c=c: h.tensor_tensor(out=ynT[:, c, :], in0=xT[:, c, T0:T0 + 512], in1=rstd[:], op=ALU.mult),
                    (("xT", blk), ("rstd", 0)), ("ynT",))
                act(lambda h, c=c: h.activation(out=ynT[:, c, :], in_=ynT[:, c, :], func=AF.Identity, scale=pfm[:, O_FG + c:O_FG + c + 1]),
                    ("ynT", "pfm"), ("ynT",))
            for t4 in range(4):
                tt = blk * 4 + t4
                yb = ytok[tt % 2]
                for half in range(2):
                    bi = psbank()
                    for c4 in range(4):
                        c = half * 4 + c4
                        pe(lambda h, bi=bi, c4=c4, c=c, t4=t4: h.transpose(psum[:, bi, c4 * 128:(c4 + 1) * 128], ynT[:, c, t4 * 128:(t4 + 1) * 128], identf[:]),
                           ("ynT", "identf"), (PK(bi),))
                    if half == 0:
                        act(lambda h, yb=yb, bi=bi: h.activation(out=yb[:, 0:512], in_=psum[:, bi, :], func=AF.Copy), (PK(bi),), (("ytok", tt % 2),))
                    else:
                        dve(lambda h, yb=yb, bi=bi: h.tensor_copy(out=yb[:, 512:1024], in_=psum[:, bi, :]), (PK(bi),), (("ytok", tt % 2),))
                P.dma("sp", y_d[tt * 128:(tt + 1) * 128, :], yb[:], reads=(("ytok", tt % 2),), writes=(okey_new("y"),))
    P.finish_waits("sp", tuple(out_keys))
    P.emit()
    es.close()
    return nc


def _fm(a, nchunk):
    a = np.asarray(a, np.float32)
    lead = a.shape[:-1]
    a = a.reshape(*lead, nchunk, 128)
    a = np.moveaxis(a, -1, 0)
    return np.ascontiguousarray(a)


def _consts():
    bf = ml_dtypes.bfloat16
    identf = np.eye(128, dtype=np.float32)
    identb = np.eye(128).astype(bf)
    sw = np.zeros((128, 128), np.float32)
    for p in range(128):
        m, d = p // 64, p % 64
        base = 32 if d >= 32 else 0
        dd = d - base
        partner = m * 64 + base + (dd + 16 if dd < 16 else dd - 16)
        sw[partner, p] = 1.0
    c = np.arange(128)
    ang = 2 * np.pi * ((c[:, None] * c[None, :]) % 128) / 128
    dftc = np.stack([np.cos(ang), -np.sin(ang)], 1) / np.sqrt(128.0)
    t = np.arange(256)
    a256 = 2 * np.pi * ((t[:, None] * t[None, :]) % 256) / 256
    C = np.cos(a256) / 16.0
    S = np.sin(a256) / 16.0
    d256 = np.stack([C.reshape(2, 128, 256), S.reshape(2, 128, 256)], 0)
    d256 = np.transpose(d256, (2, 0, 1, 3))
    return dict(identf=identf, identb=identb, swapp=sw.astype(bf), dftc=dftc.astype(bf), dft256=np.ascontiguousarray(d256).astype(bf))


def _dft4k(r):
    bf = ml_dtypes.bfloat16
    t = np.arange(4096, dtype=np.int64)
    tp = r * 1024 + np.arange(1024, dtype=np.int64)
    ang = 2 * np.pi * ((t[:, None] * tp[None, :]) % 4096) / 4096.0
    C = (np.cos(ang) / 64.0).astype(np.float32)
    S = (np.sin(ang) / 64.0).astype(np.float32)
    a = np.stack([C, S], 0).reshape(2, 32, 128, 8, 128)
    a = np.transpose(a, (3, 2, 0, 1, 4))
    return np.ascontiguousarray(a).astype(bf)


def _rope_tables(r):
    pos = r * 1024 + np.arange(1024)
    row = (pos // 64).astype(np.float32)
    col = (pos % 64).astype(np.float32)
    inv = (1.0 / (10000.0 ** (np.arange(0, 32, 2, dtype=np.float32) / 32))).astype(np.float32)
    cos = np.zeros((128, 1024), np.float32)
    sin = np.zeros((128, 1024), np.float32)
    for p in range(128):
        d = p % 64
        if d < 32:
            a = row * inv[d % 16]
            dd = d
        else:
            a = col * inv[(d - 32) % 16]
            dd = d - 32
        a = a.astype(np.float32)
        cos[p] = np.cos(a)
        sin[p] = np.sin(a) * (-1.0 if dd < 16 else 1.0)
    return cos, sin


_NC_CACHE = {}


def kernel(x_prompt, x_sample, c, cache_k, cache_v, c_ctx, norm1_g, norm2_g, final_g,
           w_ada, b_ada, w_in, w_fourier, lam_params, subln_g, w_attn, w_o,
           w_up, conv_w, conv_b, w_down, _dbg=None):
    f32 = np.float32
    A = lambda a: np.ascontiguousarray(np.asarray(a, dtype=f32))
    x_prompt, x_sample, c, cache_k, cache_v, c_ctx = map(A, (x_prompt, x_sample, c, cache_k, cache_v, c_ctx))
    w_ada, w_in, w_fourier, w_attn, w_o, w_up, w_down = map(A, (w_ada, w_in, w_fourier, w_attn, w_o, w_up, w_down))
    consts = _consts()
    pbc = np.concatenate([A(lam_params).reshape(-1), A(subln_g).reshape(-1)])[None, :].repeat(128, 0)
    pbc = np.ascontiguousarray(pbc, dtype=f32)
    in_maps = []
    for core in range(8):
        g, r = core // 4, core % 4
        xin = np.concatenate([x_sample[g, r * 1024:(r + 1) * 1024], x_prompt[4 * core:4 * core + 4].reshape(1024, D)], 0)
        pfm = np.zeros((128, NPF), f32)
        cond = np.stack([c[g], c_ctx], 0)
        pfm[:, O_COND:O_COND + 16] = np.transpose(_fm(cond, 8), (0, 2, 1)).reshape(128, 16)
        pfm[:, O_N1G:O_N1G + 32] = _fm(norm1_g, 8).reshape(128, 32)
        pfm[:, O_N2G:O_N2G + 32] = _fm(norm2_g, 8).reshape(128, 32)
        pfm[:, O_FG:O_FG + 8] = _fm(final_g, 8).reshape(128, 8)
        pfm[:, O_BADA:O_BADA + 192] = _fm(b_ada, 48).reshape(128, 192)
        pfm[:, O_CW:O_CW + 528] = _fm(conv_w, 44).reshape(128, 528)
        pfm[:, O_CB:O_CB + 176] = _fm(conv_b, 44).reshape(128, 176)
        cos, sin = _rope_tables(r)
        rope = np.ascontiguousarray(np.concatenate([cos, sin], 1))
        sel = np.zeros(8, f32)
        if r > 0:
            sel[r - 1] = 1.0
        if r < 3:
            sel[4 + r + 1] = 1.0
        pfm[:, O_SEL:O_SEL + 8] = sel[None, :]
        m = dict(xin=np.ascontiguousarray(xin), w_ada=w_ada, w_in=w_in, w_fourier=w_fourier, w_attn=w_attn, w_o=w_o,
                 w_up=w_up, w_down=w_down,
                 cache_k=np.ascontiguousarray(cache_k[g].reshape(DEPTH, 256, D)),
                 cache_v=np.ascontiguousarray(cache_v[g].reshape(DEPTH, 256, D)),
                 pfm=pfm, pbc=pbc, rope=rope, dft4k=_dft4k(r), **consts)
        in_maps.append(m)
    if _dbg is not None:
        wl = max(_dbg.get("nlayers", DEPTH), 1)
        for m in in_maps:
            for nm in ("w_ada", "w_in", "w_fourier", "w_attn", "w_o", "w_up", "w_down"):
                m[nm] = np.ascontiguousarray(m[nm][:wl])
    key = repr(_dbg)
    if key not in _NC_CACHE:
        _NC_CACHE[key] = build_program(_dbg)
    nc = _NC_CACHE[key]
    res = run_bass_kernel_spmd(nc, in_maps, core_ids=list(range(8)))
    y_prompt = np.zeros((32, 256, D), f32)
    y_sample = np.zeros((2, 4096, D), f32)
    state_k = np.zeros((32, DEPTH, 256, 8, 2, 64), f32)
    state_v = np.zeros((32, DEPTH, 256, 8, 128), f32)
    for core in range(8):
        g, r = core // 4, core % 4
        o = res.results[core]
        y = np.asarray(o["y"], f32)
        y_sample[g, r * 1024:(r + 1) * 1024] = y[0:1024]
        y_prompt[4 * core:4 * core + 4] = y[1024:2048].reshape(4, 256, D)
        state_k[4 * core:4 * core + 4] = np.asarray(o["sk"], f32).reshape(4, DEPTH, 256, 8, 2, 64)
        state_v[4 * core:4 * core + 4] = np.asarray(o["sv"], f32).reshape(4, DEPTH, 256, 8, 128)
    return (y_prompt, y_sample, state_k, state_v)
```

```python
import math
import numpy as np
import ml_dtypes
import concourse.bass as bass
import concourse.mybir as mybir
from concourse.bass_utils import run_bass_kernel_spmd

F32 = mybir.dt.float32
BF16 = mybir.dt.bfloat16
AF = mybir.ActivationFunctionType
ALU = mybir.AluOpType
AX = mybir.AxisListType

D = 1024
DEPTH = 4
DFF = 2816
NJ = 22
INW = 5632
EPS = 1e-6
NTOK = 2048
LAM_INIT = [0.8 - 0.6 * math.exp(-0.3 * l) for l in range(DEPTH)]

O_COND = 0
O_N1G = O_COND + 16
O_N2G = O_N1G + 32
O_FG = O_N2G + 32
O_BADA = O_FG + 8
O_CW = O_BADA + 192
O_CB = O_CW + 528
O_COS = O_CB + 176
O_SIN = O_COS + 1024
O_SEL = O_COS
NPF = O_SEL + 8
NPB = 1024 + 512

DEBUG_STOP = None


import types


def _freeze(fn):
    if fn is None or fn.__closure__ is None:
        return fn
    cells = []
    for c in fn.__closure__:
        try:
            cells.append(types.CellType(c.cell_contents))
        except ValueError:
            cells.append(c)
    return types.FunctionType(fn.__code__, fn.__globals__, fn.__name__, fn.__defaults__, tuple(cells))


class Prog:
    def __init__(self, nc):
        self.nc = nc
        self.eng = {}
        for name, h in (("pe", nc.tensor), ("act", nc.scalar), ("dve", nc.vector), ("pool", nc.gpsimd), ("sp", nc.sync)):
            self.eng[name] = dict(h=h, sem=None, cnt=0, ops=[], waited={})
        self.lanes = {}
        self.lane_rr = {}
        self.last_w = {}
        self.readers = {}
        self.sems = []

    def new_sem(self, name):
        s = self.nc.alloc_semaphore(name=name)
        self.sems.append(s)
        return s

    def setup(self, n_sp=8, n_pool=8, n_cc=6):
        for name in self.eng:
            self.eng[name]["sem"] = self.new_sem("e_" + name)
        for q, n in (("sp", n_sp), ("pool", n_pool)):
            self.lanes[q] = [dict(sem=self.new_sem(f"l_{q}{i}"), cnt=0, unit=16, id=(q, i)) for i in range(n)]
            self.lane_rr[q] = 0
        self.lanes["cc"] = [dict(sem=self.new_sem(f"l_cc{i}"), cnt=0, unit=1, id=("cc", i)) for i in range(n_cc)]
        self.lane_rr["cc"] = 0

    @staticmethod
    def _norm_keys(keys):
        return tuple(("ps", 7) if (isinstance(k, tuple) and len(k) == 2 and k[0] == "psb") else k for k in keys)

    def _deps(self, reads, writes, me=None):
        deps = {}

        def add(tok):
            if tok is None:
                return
            k, v = tok
            if deps.get(k, 0) < v:
                deps[k] = v

        for r in reads:
            add(self.last_w.get(r))
            if isinstance(r, tuple) and len(r) == 2 and r[0] == "ps":
                for k, v in self.readers.get(r, {}).items():
                    if k != me:
                        add((k, v))
        for w in writes:
            add(self.last_w.get(w))
            for k, v in self.readers.get(w, {}).items():
                add((k, v))
        return deps

    def _commit(self, tok, reads, writes):
        k, v = tok
        for r in reads:
            d = self.readers.setdefault(r, {})
            if d.get(k, 0) < v:
                d[k] = v
        for w in writes:
            self.last_w[w] = tok
            self.readers[w] = {}

    def _waits(self, ename, deps):
        e = self.eng[ename]
        out = []
        items = sorted(deps.items(), key=lambda kv: 0 if kv[0] == ("eng", ename) else 1)
        for k, v in items:
            if k == ("eng", "pe") and ename == "pe":
                continue
            if e["waited"].get(k, 0) >= v:
                continue
            e["waited"][k] = v
            if k[0] == "eng":
                out.append((self.eng[k[1]]["sem"], v))
            else:
                ln = self.lanes[k[1][0]][k[1][1]]
                out.append((ln["sem"], v * ln["unit"]))
        return out

    def op(self, ename, fn, reads=(), writes=()):
        e = self.eng[ename]
        reads, writes = self._norm_keys(reads), self._norm_keys(writes)
        deps = self._deps(reads, writes, me=("eng", ename))
        waits = self._waits(ename, deps)
        e["cnt"] += 1
        tok = (("eng", ename), e["cnt"])
        e["ops"].append((waits, _freeze(fn), (e["sem"], 1)))
        self._commit(tok, reads, writes)

    def _lane_op(self, q, lanekind, fn, reads, writes):
        lanes = self.lanes[lanekind]
        i = self.lane_rr[lanekind]
        self.lane_rr[lanekind] = (i + 1) % len(lanes)
        ln = lanes[i]
        deps = self._deps(reads, writes)
        if ln["cnt"] > 0:
            deps[("lane", ln["id"])] = max(deps.get(("lane", ln["id"]), 0), ln["cnt"])
        waits = self._waits(q, deps)
        ln["cnt"] += 1
        tok = (("lane", ln["id"]), ln["cnt"])
        self.eng[q]["ops"].append((waits, _freeze(fn), (ln["sem"], ln["unit"])))
        self._commit(tok, reads, writes)

    def dma(self, q, out, in_, reads=(), writes=()):
        self._lane_op(q, q, lambda h: h.dma_start(out=out, in_=in_), reads, writes)

    def cc(self, fn, reads=(), writes=()):
        self._lane_op("pool", "cc", fn, reads, writes)

    def barrier(self):
        deps = {}
        for name, e in self.eng.items():
            if e["cnt"] > 0:
                deps[("eng", name)] = e["cnt"]
        for q, ls in self.lanes.items():
            if q == "cc":
                continue
            for ln in ls:
                if ln["cnt"] > 0:
                    deps[("lane", ln["id"])] = ln["cnt"]
        for name in self.eng:
            d = dict(deps)
            w = []
            e = self.eng[name]
            for k, v in d.items():
                if e["waited"].get(k, 0) >= v:
                    continue
                e["waited"][k] = v
                if k[0] == "eng":
                    w.append((self.eng[k[1]]["sem"], v))
                else:
                    ln = self.lanes[k[1][0]][k[1][1]]
                    w.append((ln["sem"], v * ln["unit"]))
            if w:
                e["ops"].append((w, None, None))

    def finish_waits(self, ename, keys):
        deps = self._deps(keys, ())
        waits = self._waits(ename, deps)
        self.eng[ename]["ops"].append((waits, None, None))

    def emit(self):
        nc = self.nc
        with nc.Block() as block:
            for name, deco in (("pe", block.tensor), ("act", block.scalar), ("dve", block.vector),
                               ("pool", block.gpsimd), ("sp", block.sync)):
                ops = self.eng[name]["ops"]

                def body(h, ops=ops):
                    for waits, fn, inc in ops:
                        for sem, val in waits:
                            h.wait_ge(sem, val)
                        if fn is not None:
                            ins = fn(h)
                            ins.then_inc(inc[0], inc[1])

                deco(body)


def build_program(dbg=None):
    nc = bass.Bass("TRN2", target_bir_lowering=False)
    WL = DEPTH if dbg is None else dbg.get("nlayers", DEPTH)
    WL = max(WL, 1)
    _orig_sbuf_tensor = nc.sbuf_tensor
    _uniq = [0]

    def _sbuf_tensor(name, shape, dt):
        _uniq[0] += 1
        return _orig_sbuf_tensor(f"{name}_u{_uniq[0]}", shape, dt)
    P = Prog(nc)
    P.setup()

    def din(name, shape, dt=F32):
        return nc.dram_tensor(name, list(shape), dt, kind="ExternalInput").ap()

    def dout(name, shape, dt=F32):
        return nc.dram_tensor(name, list(shape), dt, kind="ExternalOutput").ap()

    xin = din("xin", [NTOK, D])
    w_ada = din("w_ada", [WL, D, 6 * D])
    w_in = din("w_in", [WL, D, INW])
    w_fourier = din("w_fourier", [WL, 512, D])
    w_attn = din("w_attn", [WL, D, D])
    w_o = din("w_o", [WL, D, D])
    w_up = din("w_up", [WL, D, 2 * DFF])
    w_down = din("w_down", [WL, DFF, D])
    cache_k = din("cache_k", [DEPTH, 256, D])
    cache_v = din("cache_v", [DEPTH, 256, D])
    pfm_d = din("pfm", [128, NPF])
    rope_d = din("rope", [128, 2048])
    pbc_d = din("pbc", [128, NPB])
    identf_d = din("identf", [128, 128])
    identb_d = din("identb", [128, 128], BF16)
    swapp_d = din("swapp", [128, 128], BF16)
    dftc_d = din("dftc", [128, 2, 128], BF16)
    dft256_d = din("dft256", [128, 2, 2, 256], BF16)
    dft4k_d = din("dft4k", [8, 128, 2, 32, 128], BF16)

    y_d = dout("y", [NTOK, D])
    sk_d = dout("sk", [4, DEPTH, 256, D])
    sv_d = dout("sv", [4, DEPTH, 256, D])

    def dint(name, shape, dt=BF16):
        return nc.dram_tensor(name, list(shape), dt).ap()

    ib_k_raw = [[dint(f"ibk{l}_{j}", [1024, 512]) for j in range(2)] for l in range(DEPTH)]
    ob_k_raw = [[dint(f"obk{l}_{j}", [4096, 512]) for j in range(2)] for l in range(DEPTH)]
    ib_k = [[a.rearrange("(f h) n -> f (h n)", h=2) for a in row] for row in ib_k_raw]
    ob_k = [[a.rearrange("(f h) n -> f (h n)", h=2) for a in row] for row in ob_k_raw]
    ib_v = [[dint(f"ibv{l}_{j}", [1024, 512]) for j in range(2)] for l in range(DEPTH)]
    ob_v = [[dint(f"obv{l}_{j}", [4096, 512]) for j in range(2)] for l in range(DEPTH)]
    ib_f = [dint(f"ibf{l}", [1024, 512]) for l in range(DEPTH)]
    ob_f = [dint(f"obf{l}", [4096, 512]) for l in range(DEPTH)]
    ib_h = [dint(f"ibh{l}", [128, 128]) for l in range(DEPTH)]
    ob_h = [dint(f"obh{l}", [512, 128]) for l in range(DEPTH)]
    RG = [[0, 1, 2, 3], [4, 5, 6, 7]]

    from contextlib import ExitStack
    es = ExitStack()

    def sb(name, shape, dt=F32):
        return es.enter_context(_sbuf_tensor(name, list(shape), dt))

    xT = sb("xT", [128, 8, NTOK])
    pfm = sb("pfm_sb", [128, NPF])
    identf = sb("identf_sb", [128, 128])
    identb = sb("identb_sb", [128, 128], BF16)
    swapp = sb("swapp_sb", [128, 128], BF16)
    dftc = sb("dftc_sb", [128, 2, 128], BF16)
    dft256 = sb("dft256_sb", [128, 2, 2, 256], BF16)
    onesb = sb("onesb", [128, 128], BF16)
    mods = sb("mods", [128, DEPTH, 48, 2])
    modA = sb("modA", [128, DEPTH, 2, 2, 8])
    lamv = sb("lamv", [128, DEPTH, 2])
    subg = sb("subg", [128, DEPTH, 128])
    scT = sb("scT", [128, 8, 2], BF16)
    NWB = 3
    wbufs = [sb(f"wbuf{i}", [128, 4096], BF16) for i in range(NWB)]
    psum = es.enter_context(nc.psum_tensor("psum", [128, 8, 512], F32))

    ps_rr = [0]

    def psbank(n=7, base=0):
        i = base + ps_rr[0] % n
        ps_rr[0] += 1
        return i

    def PK(i):
        return ("ps", i)

    wstate = dict(issued=0, specs=[])

    def wspec_add(src_ap, shape, view):
        wstate["specs"].append((src_ap, shape, view))
        return len(wstate["specs"]) - 1

    def w_issue_upto(n):
        while wstate["issued"] < min(n, len(wstate["specs"])):
            i = wstate["issued"]
            src, shape, view = wstate["specs"][i]
            buf = wbufs[i % NWB]
            dst = view(buf)
            P.dma("pool", dst, src, reads=(), writes=(("w", i % NWB),))
            wstate["issued"] += 1

    def w_get(i):
        w_issue_upto(i + NWB - 1 + 1 - 0)
        src, shape, view = wstate["specs"][i]
        return view(wbufs[i % NWB]), ("w", i % NWB)

    def wtile(src_ap, view):
        i = wspec_add(src_ap, None, view)
        w_issue_upto(i + 1)
        return view(wbufs[i % NWB]), ("w", i % NWB)

    class WQ:
        def __init__(self):
            self.plan = []
            self.pos = 0
            self.issued = 0

        def add(self, src_ap, view):
            self.plan.append((src_ap, view))

        def _issue(self, upto):
            while self.issued < min(upto, len(self.plan)):
                src, view = self.plan[self.issued]
                slot = self.issued % NWB
                if isinstance(src, list):
                    for si, (s_ap, s_view) in enumerate(src):
                        P.dma("pool", s_view(wbufs[slot]), s_ap, reads=(), writes=(("w", slot),))
                else:
                    P.dma("pool", view(wbufs[slot]), src, reads=(), writes=(("w", slot),))
                self.issued += 1

        def next(self):
            i = self.pos
            self._issue(i + NWB - 1)
            if self.issued <= i:
                self._issue(i + 1)
            self.pos += 1
            src, view = self.plan[i]
            slot = i % NWB
            return view(wbufs[slot]), ("w", slot)

        def keys(self, slot):
            return (("w", slot, 0), ("w", slot, 1))

        def prefetch(self):
            self._issue(self.pos + NWB - 1)

    WQ_ = WQ()

    def v_k512(buf):
        return buf[:, 0:4096].rearrange("p (k n) -> p k n", k=8)

    def v_k4_512(buf):
        return buf[:, 0:2048].rearrange("p (k n) -> p k n", k=4)

    def v_up(buf):
        return buf[:, 0:4096].rearrange("p (k v n) -> p k v n", k=8, v=2)

    def v_down(buf):
        return buf[:, 0:2816].rearrange("p (k n) -> p k n", k=22)

    def src_k512(w, l, c0, kchunks=8):
        return w[l, :, c0:c0 + 512].rearrange("(k p) n -> p k n", p=128)

    def plan_weights():
        def plan_ada(l):
            for nt in range(12):
                WQ_.add(src_k512(w_ada, l, nt * 512), v_k512)
        plan_ada(0)
        def plan_a(l, g):
            for t in ([0, 1, 2, 3, 4, 5, 6] if g == 0 else [1, 2, 3, 4, 5, 6, 0]):
                WQ_.add(src_k512(w_in, l, t * 512), v_k512)

        def plan_b(l):
            for hh in range(2):
                WQ_.add(src_k512(w_in, l, 3584 + hh * 512), v_k512)
                WQ_.add(w_fourier[l, :, hh * 512:(hh + 1) * 512].rearrange("(k p) n -> p k n", p=128), v_k4_512)
            for hh in range(2):
                WQ_.add(src_k512(w_in, l, 4608 + hh * 512), v_k512)
                WQ_.add(src_k512(w_attn, l, hh * 512), v_k512)
            for hh in range(2):
                WQ_.add(src_k512(w_o, l, hh * 512), v_k512)

        def plan_ffn(l):
            for jj in range(11):
                srcs = []
                for vgi in range(2):
                    s_ap = w_up[l][:, vgi * DFF + jj * 256:vgi * DFF + (jj + 1) * 256].rearrange("(k p) n -> p k n", p=128)
                    srcs.append((s_ap, (lambda buf, vgi=vgi: v_up(buf)[:, :, vgi, :])))
                WQ_.add(srcs, v_up)
            for fo in range(8):
                WQ_.add(w_down[l, :, fo * 128:(fo + 1) * 128].rearrange("(k p) n -> p k n", p=128), v_down)

        for l in range(WL):
            plan_a(l, 0)
            if l > 0:
                plan_ffn(l - 1)
            if l + 1 < WL:
                plan_ada(l + 1)
            plan_b(l)
            plan_ffn(l)
            plan_a(l, 1)
            plan_b(l)
        plan_ffn(WL - 1)

    plan_weights()

    def act(fn, reads, writes):
        P.op("act", fn, reads, writes)

    def dve(fn, reads, writes):
        P.op("dve", fn, reads, writes)

    def pe(fn, reads, writes):
        P.op("pe", fn, reads, writes)

    def mm(out, lhsT, rhs, start, stop, reads, writes, **kw):
        pe(lambda h: h.matmul(out, lhsT=lhsT, rhs=rhs, start=start, stop=stop, **kw), reads, writes)

    P.dma("sp", pfm[:], pfm_d[:, :], writes=("pfm",))
    P.dma("sp", identf[:], identf_d[:, :], writes=("identf",))
    P.dma("sp", identb[:], identb_d[:, :], writes=("identb",))
    P.dma("sp", swapp[:], swapp_d[:, :], writes=("swapp",))
    P.dma("sp", dftc[:], dftc_d[:, :, :], writes=("dftc",))
    P.dma("sp", dft256[:], dft256_d[:, :, :, :], writes=("dft256",))
    dve(lambda h: h.memset(onesb[:], 1.0), (), ("onesb",))

    act(lambda h: h.activation(out=scT[:], in_=pfm[:, O_COND:O_COND + 16].rearrange("p (k c) -> p k c", k=8), func=AF.Silu),
        ("pfm",), ("scT",))

    with _sbuf_tensor("xtok0", [128, D], F32) as xtok0, _sbuf_tensor("xtok1", [128, D], F32) as xtok1:
        xtoks = [xtok0, xtok1]
        for tt in range(16):
            xb = xtoks[tt % 2]
            P.dma("sp", xb[:], xin[tt * 128:(tt + 1) * 128, :], writes=(("xtok", tt % 2),))
            for half in range(2):
                bi = psbank()
                for c4 in range(4):
                    c = half * 4 + c4
                    pe(lambda h, bi=bi, c4=c4, c=c, xb=xb: h.transpose(psum[:, bi, c4 * 128:(c4 + 1) * 128], xb[:, c * 128:(c + 1) * 128], identf[:]),
                       (("xtok", tt % 2), "identf"), (PK(bi),))
                dstv = xT[:, half * 4:(half + 1) * 4, tt * 128:(tt + 1) * 128]
                srcv = psum[:, bi, :].rearrange("p (c t) -> p c t", c=4)
                if (tt + half) % 2 == 0:
                    act(lambda h, dstv=dstv, srcv=srcv: h.activation(out=dstv, in_=srcv, func=AF.Copy), (PK(bi),), (("xT", tt // 4),))
                else:
                    dve(lambda h, dstv=dstv, srcv=srcv: h.tensor_copy(out=dstv, in_=srcv), (PK(bi),), (("xT", tt // 4),))

    P.barrier()

    def compute_mods(l):
        for nt in range(12):
            wt, wk = WQ_.next()
            bi = psbank()
            for c in range(4):
                for k in range(8):
                    mm(psum[:, bi, c * 2:c * 2 + 2], wt[:, k, c * 128:(c + 1) * 128], scT[:, k, :], k == 0, k == 7,
                       (wk, "scT"), (PK(bi),))
            for c in range(4):
                ch = nt * 4 + c
                dve(lambda h, bi=bi, c=c, ch=ch, l=l: h.tensor_scalar(out=mods[:, l, ch, :], in0=psum[:, bi, c * 2:c * 2 + 2],
                                                                      scalar1=pfm[:, O_BADA + l * 48 + ch:O_BADA + l * 48 + ch + 1],
                                                                      scalar2=None, op0=ALU.add),
                    (PK(bi), "pfm"), ("mods",))
        for cond in range(2):
            for which in range(2):
                j = 1 if which == 0 else 4
                ng = O_N1G if which == 0 else O_N2G
                dve(lambda h, l=l, cond=cond, which=which, j=j, ng=ng: h.scalar_tensor_tensor(
                    out=modA[:, l, cond, which, :], in0=mods[:, l, j * 8:(j + 1) * 8, cond], scalar=1.0,
                    in1=pfm[:, ng + l * 8:ng + (l + 1) * 8], op0=ALU.add, op1=ALU.mult), ("mods", "pfm"), ("modA",))

    compute_mods(0)
    with _sbuf_tensor("lamtmp", [128, 64], F32) as lamtmp, _sbuf_tensor("lams", [128, 2], F32) as lams, \
            _sbuf_tensor("pbc_sb", [128, NPB], F32) as pbc:
        P.dma("sp", pbc[:], pbc_d[:, :], writes=("pbc",))
        for l in range(DEPTH):
            for i in range(2):
                a0 = l * 256 + (2 * i) * 64
                dve(lambda h, a0=a0: h.tensor_tensor(out=lamtmp[:], in0=pbc[:, a0:a0 + 64], in1=pbc[:, a0 + 64:a0 + 128], op=ALU.mult),
                    ("pbc",), ("lamtmp",))
                dve(lambda h, i=i: h.reduce_sum(out=lams[:, i:i + 1], in_=lamtmp[:], axis=AX.X), ("lamtmp",), ("lams",))
            act(lambda h: h.activation(out=lams[:], in_=lams[:], func=AF.Exp), ("lams",), ("lams",))
            dve(lambda h, l=l: h.scalar_tensor_tensor(out=lamv[:, l, 0:1], in0=lams[:, 0:1], scalar=LAM_INIT[l], in1=lams[:, 1:2],
                                                      op0=ALU.add, op1=ALU.subtract), ("lams",), ("lamv",))
            dve(lambda h, l=l: h.tensor_scalar(out=lamv[:, l, 1:2], in0=lamv[:, l, 0:1], scalar1=-1.0, scalar2=None, op0=ALU.mult),
                ("lamv",), ("lamv",))
            dve(lambda h, l=l: h.tensor_scalar(out=subg[:, l, :], in0=pbc[:, 1024 + l * 128:1024 + (l + 1) * 128],
                                               scalar1=1.0 - LAM_INIT[l], scalar2=None, op0=ALU.mult), ("pbc",), ("subg",))

    P.barrier()
    def rstd_block(T0, rstd, sq, tag=0):
        blk = ("xT", T0 // 512)
        act(lambda h: h.activation(out=sq[:], in_=xT[:, :, T0:T0 + 512], func=AF.Square), (blk,), (("sq", tag),))
        bi = psbank()
        for c in range(8):
            mm(psum[:, bi, :], onesb[:], sq[:, c, :], c == 0, c == 7, (("sq", tag), "onesb"), (PK(bi),))
        dve(lambda h: h.tensor_scalar(out=rstd[:], in0=psum[:, bi, :], scalar1=1.0 / D, scalar2=EPS, op0=ALU.mult, op1=ALU.add),
            (PK(bi),), (("rstd", tag),))
        act(lambda h: h.activation(out=rstd[:], in_=rstd[:], func=AF.Ln), (("rstd", tag),), (("rstd", tag),))
        act(lambda h: h.activation(out=rstd[:], in_=rstd[:], func=AF.Exp, scale=-0.5), (("rstd", tag),), (("rstd", tag),))

    def norm_mod(grp, l, which, hnT, scr):
        rstds, sqs, tmp2 = scr
        cond = grp["cond"]
        shj = 0 if which == 0 else 3
        for b in range(2):
            rstd_block(grp["tok0"] + b * 512, rstds[b], sqs[b], tag=b)
        for b in range(2):
            T0 = grp["tok0"] + b * 512
            rstd = rstds[b]
            for c in range(8):
                t = tmp2[c % 2]
                dve(lambda h, c=c, t=t: h.tensor_tensor(out=t[:], in0=xT[:, c, T0:T0 + 512], in1=rstd[:], op=ALU.mult),
                    (("xT", T0 // 512), ("rstd", b)), (("ntmp", c % 2),))
                act(lambda h, c=c, t=t: h.activation(out=hnT[:, c, b * 512:(b + 1) * 512], in_=t[:], func=AF.Identity,
                                                     scale=modA[:, l, cond, which, c:c + 1],
                                                     bias=mods[:, l, shj * 8 + c, cond:cond + 1]),
                    (("ntmp", c % 2), "modA", "mods"), (("hnT", b),))

    def lin_fm(wt, wk, nk, actT, actkeys, ncols_chunks, epi, chunk0=0):
        pending = None
        for c in range(ncols_chunks):
            for b in range(2):
                bi = psbank()
                for k in range(nk):
                    mm(psum[:, bi, :], wt[:, k, c * 128:(c + 1) * 128], actT[:, k, b * 512:(b + 1) * 512], k == 0, k == nk - 1,
                       (wk, actkeys(b)), (PK(bi),))
                if pending is not None:
                    pending()
                pending = epi(chunk0 + c, b, bi)
        if pending is not None:
            pending()

    def lin_tm(wt, wk, nk, actT, actkeys, epi):
        for tt in range(8):
            bi = psbank()
            for k in range(nk):
                mm(psum[:, bi, :], actT[:, k, tt * 128:(tt + 1) * 128], wt[:, k, :], k == 0, k == nk - 1,
                   (wk, actkeys(tt // 4)), (PK(bi),))
            epi(tt, bi)

    GS = dict(id=0, tok0=0, cond=0, nseq=1, L=1024, rope=True)
    GP = dict(id=1, tok0=1024, cond=1, nseq=4, L=256, rope=False)

    hnT = sb("hnT", [128, 8, 1024], BF16)
    hk = lambda b: ("hnT", b)
    out_keys = []

    def okey_new(name):
        k = (name, len(out_keys))
        out_keys.append(k)
        return k

    def agather(src, dst, rkeys, wkey):
        P.cc(lambda h: h.collective_compute("AllGather", ALU.bypass, replica_groups=RG, ins=[src.opt()], outs=[dst.opt()]),
             reads=tuple(rkeys), writes=(wkey,))

    def proj_f(grp, l, st):
        wt, wk = WQ_.next()

        def epi_f(tt, bi):
            if grp["rope"]:
                dst = st["stage_b"][tt % 2]
                act(lambda h: h.activation(out=dst[:], in_=psum[:, bi, :], func=AF.Copy), (PK(bi),), (("stgb", tt % 2),))
                P.dma("sp", ib_f[l][tt * 128:(tt + 1) * 128, :], dst[:], reads=(("stgb", tt % 2),), writes=(("ibf", l, tt),))
            else:
                act(lambda h: h.activation(out=st["ftok"][:, tt, :], in_=psum[:, bi, :], func=AF.Copy), (PK(bi),), (("ftok", tt // 2),))
        lin_tm(wt, wk, 8, hnT, hk, epi_f)
        if grp["rope"] and not (dbg or {}).get("nocc"):
            agather(ib_f[l], ob_f[l], [("ibf", l, tt) for tt in range(8)], ("obf", l))

    def rope_epi(st, dstT, dkey):
        cosv = st["rope"][:, 0:1024]
        sinv = st["rope"][:, 1024:2048]

        lvl = (dbg or {}).get("ropelvl", 5)

        def epi(hd, b, bi):
            qb = st["ropeb"][b % 2]
            act(lambda h: h.activation(out=qb[:], in_=psum[:, bi, :], func=AF.Copy), (PK(bi),), (("ropeb", b % 2),))
            def tail():
                b2 = psbank()
                mm(psum[:, b2, :], swapp[:], qb[:], True, True, (("ropeb", b % 2), "swapp"), (PK(b2),))
                t1 = st["ropet"][0]
                t2 = st["ropet"][1]
                dve(lambda h: h.tensor_tensor(out=t1[:], in0=psum[:, bi, :], in1=cosv[:, b * 512:(b + 1) * 512], op=ALU.mult),
                    (PK(bi), "rope"), (("ropet", 0),))
                dve(lambda h: h.tensor_tensor(out=t2[:], in0=psum[:, b2, :], in1=sinv[:, b * 512:(b + 1) * 512], op=ALU.mult),
                    (PK(b2), "rope"), (("ropet", 1),))
                dve(lambda h: h.tensor_tensor(out=dstT[:, hd, b * 512:(b + 1) * 512], in0=t1[:], in1=t2[:], op=ALU.add),
                    (("ropet", 0), ("ropet", 1)), (dkey(hd, b),))
            return tail
        return epi

    def plain_epi(dstT, dkey):
        def epi(hd, b, bi):
            if (dbg or {}).get("noepi"):
                return
            if (dbg or {}).get("epi2stg"):
                dst = stg_dbg[b % 2]
                act(lambda h: h.activation(out=dst[:], in_=psum[:, bi, :], func=AF.Copy), (PK(bi),), (("stgb", b % 2),))
                return
            act(lambda h: h.activation(out=dstT[:, hd, b * 512:(b + 1) * 512], in_=psum[:, bi, :], func=AF.Copy),
                (PK(bi),), (dkey(hd, b),))
        return epi

    stop = None if dbg is None else dbg.get("stop")

    stg_dbg = []

    def proj_qkv(grp, l, st, qT):
        stg_dbg[:] = st["stage_b"]
        qkey = (lambda hd, b: ("qT", hd, b)) if (dbg or {}).get("qkeyu") else (lambda hd, b: ("qT",))
        for t in range(2):
            wt, wk = WQ_.next()
            lin_fm(wt, wk, 8, hnT, hk, 4, rope_epi(st, qT, qkey) if (grp["rope"] and not (dbg or {}).get("norope")) else plain_epi(qT, qkey), chunk0=t * 4)
        if stop == "projq":
            return
        if grp["rope"]:
            kTs = st["kTs"]
            kkey = lambda hd, b: ("kTs", hd // 4)
            for t in range(2):
                wt, wk = WQ_.next()
                lin_fm(wt, wk, 8, hnT, hk, 4, rope_epi(st, kTs, kkey), chunk0=t * 4)
                for hq in range(4):
                    hd = t * 4 + hq
                    P.dma("sp", ib_k[l][t][hq * 128:(hq + 1) * 128, :], kTs[:, hd, :], reads=(("kTs", t),), writes=(("ibk", l, t, hq),))
                agather(ib_k_raw[l][t], ob_k_raw[l][t], [("ibk", l, t, hq) for hq in range(4)], ("obk", l, t))
        else:
            kT = st["kT"]
            for t in range(2):
                wt, wk = WQ_.next()

                def epi_k(tt, bi, t=t):
                    s_, t0 = tt // 2, (tt % 2) * 128
                    stg = st["stage_f"][tt % 2]
                    act(lambda h: h.activation(out=stg[:], in_=psum[:, bi, :], func=AF.Copy), (PK(bi),), (("stgf", tt % 2),))
                    P.dma("sp", sk_d[s_, l, t0:t0 + 128, t * 512:(t + 1) * 512], stg[:], reads=(("stgf", tt % 2),), writes=(okey_new("sk"),))
                    nb_ = len(st["stage_b"])
                    kb = st["stage_b"][tt % nb_]
                    dve(lambda h: h.tensor_copy(out=kb[:], in_=psum[:, bi, :]), (PK(bi),), (("stgb", tt % nb_),))
                    pb = tt % 2
                    psb = psum[:, 7, :].bitcast(BF16)[:, pb * 512:(pb + 1) * 512]
                    for hq in range(4):
                        pe(lambda h, hq=hq: h.transpose(psb[:, hq * 128:(hq + 1) * 128], kb[:, hq * 128:(hq + 1) * 128], identb[:]),
                           (("stgb", tt % nb_), "identb"), (("psb", pb),))
                    dve(lambda h: h.tensor_copy(out=kT[:, t * 4:(t + 1) * 4, tt * 128:(tt + 1) * 128],
                                                in_=psb.rearrange("p (c t) -> p c t", c=4)), (("psb", pb),), (("kT", tt // 2),))
                lin_tm(wt, wk, 8, hnT, hk, epi_k)
        for t in range(2):
            wt, wk = WQ_.next()

            def epi_v(tt, bi, t=t):
                if grp["rope"]:
                    dst = st["stage_b"][tt % 2]
                    act(lambda h: h.activation(out=dst[:], in_=psum[:, bi, :], func=AF.Copy), (PK(bi),), (("stgb", tt % 2),))
                    P.dma("sp", ib_v[l][t][tt * 128:(tt + 1) * 128, :], dst[:], reads=(("stgb", tt % 2),), writes=(("ibv", l, t, tt),))
                else:
                    s_, t0 = tt // 2, (tt % 2) * 128
                    stg = st["stage_f"][tt % 2]
                    act(lambda h: h.activation(out=stg[:], in_=psum[:, bi, :], func=AF.Copy), (PK(bi),), (("stgf", tt % 2),))
                    P.dma("sp", sv_d[s_, l, t0:t0 + 128, t * 512:(t + 1) * 512], stg[:], reads=(("stgf", tt % 2),), writes=(okey_new("sv"),))
                    dve(lambda h: h.tensor_copy(out=st["vaug"][:, tt, t * 4:(t + 1) * 4, 0:128],
                                                in_=psum[:, bi, :].rearrange("p (c d) -> p c d", c=4)), (PK(bi),), (("vaug", tt // 2),))
            lin_tm(wt, wk, 8, hnT, hk, epi_v)
            if grp["rope"]:
                agather(ib_v[l][t], ob_v[l][t], [("ibv", l, t, tt) for tt in range(8)], ("obv", l, t))

    def attn_block(l, qTh, NQ, nkc, kT_of, v_of, kv_keys, st, otok, otile0, okey, h_idx):
        nqt = NQ // 128
        E = st["E"]
        nE = len(E)

        def qk(kc):
            sb_ = 2 * (kc % 2)
            kTa = kT_of(kc)
            mm(psum[:, sb_, 0:NQ], kTa[0:64, :], qTh[0:64, :], True, True, kv_keys + (("qT",),), (PK(sb_),))
            mm(psum[:, sb_ + 1, 0:NQ], kTa[64:128, :], qTh[64:128, :], True, True, kv_keys + (("qT",),), (PK(sb_ + 1),))

        def ex(kc):
            sb_ = 2 * (kc % 2)
            Eb = E[kc % nE]
            act(lambda h, Eb=Eb, sb_=sb_: h.activation(out=Eb[:, :, 0:NQ], in_=psum[:, sb_:sb_ + 2, 0:NQ], func=AF.Exp, scale=0.125),
                (PK(sb_), PK(sb_ + 1)), (("E", kc % nE),))

        def pv(kc):
            Eb = E[kc % nE]
            va = v_of(kc)
            for qt in range(nqt):
                for m in range(2):
                    idx = qt * 2 + m
                    bank = 4 + idx // 3
                    col = (idx % 3) * 129
                    first = (kc == 0) and (idx % 3 == 0)
                    mm(psum[:, bank, col:col + 129], Eb[:, m, qt * 128:(qt + 1) * 128], va, first, kc == nkc - 1,
                       (("E", kc % nE),) + kv_keys, (PK(bank),), skip_group_check=True)

        qk(0)
        for kc in range(nkc):
            if kc + 1 < nkc:
                qk(kc + 1)
            ex(kc)
            pv(kc)
        fin = st["fin"]
        nacc = 2 * nqt
        for bank in range(4, 4 + (nacc + 2) // 3):
            a_lo = 3 * (bank - 4)
            n_in = min(3, nacc - a_lo)
            dve(lambda h, bank=bank, a_lo=a_lo, n_in=n_in: h.tensor_copy(
                out=fin[:, a_lo:a_lo + n_in, :], in_=psum[:, bank, 0:n_in * 129].rearrange("p (a c) -> p a c", c=129)),
                (PK(bank),), ("fin",))
        fin_math(l, st, nqt, otok, otile0, okey, h_idx)

    def fin_math(l, st, nqt, otok, otile0, okey, h_idx):
        rec = st["rec"]
        obuf = st["o32"]
        ssb = st["ssb"]
        fin = st["fin"]
        NT = st["NT"]
        sk_ = st.get("ssbk", "ssb")
        fv = fin[:, 0:2 * nqt, :].rearrange("p (q m) c -> p q m c", m=2)
        o = obuf[:, 0:nqt, :]
        c0 = h_idx * NT + otile0
        r0 = rec[:, 0:nqt]
        r1 = rec[:, 4:4 + nqt]
        dve(lambda h: h.reciprocal(out=r0.unsqueeze(2), in_=fv[:, :, 0, 128:129]), ("fin",), ("rec",))
        dve(lambda h: h.reciprocal(out=r1.unsqueeze(2), in_=fv[:, :, 1, 128:129]), ("fin",), ("rec",))
        dve(lambda h: h.tensor_scalar(out=r1, in0=r1, scalar1=lamv[:, l, 1:2], scalar2=None, op0=ALU.mult), ("rec", "lamv"), ("rec",))
        dve(lambda h: h.tensor_tensor(out=o, in0=fv[:, :, 0, 0:128], in1=r0.unsqueeze(2).broadcast_to([128, nqt, 128]), op=ALU.mult),
            ("fin", "rec"), ("o32",))
        dve(lambda h: h.tensor_tensor(out=fv[:, :, 1, 0:128], in0=fv[:, :, 1, 0:128], in1=r1.unsqueeze(2).broadcast_to([128, nqt, 128]), op=ALU.mult),
            ("fin", "rec"), ("fin",))
        dve(lambda h: h.tensor_tensor(out=o, in0=o, in1=fv[:, :, 1, 0:128], op=ALU.add), ("o32", "fin"), ("o32",))
        dve(lambda h: h.tensor_tensor(out=fv[:, :, 0, 0:128], in0=o, in1=o, op=ALU.mult), ("o32", "fin"), ("fin",))
        dve(lambda h: h.reduce_sum(out=ssb[:, c0:c0 + nqt], in_=fv[:, :, 0, 0:128], axis=AX.X), ("fin",),
            tuple((sk_, c) for c in range(c0, c0 + nqt)))
        dve(lambda h: h.tensor_copy(out=otok[:, otile0:otile0 + nqt, h_idx * 128:(h_idx + 1) * 128], in_=o), ("o32",), (okey,))

    def attn_prompt(l, st, qT, otok2, ssbs, fin_seq):
        kT, vaug, E = st["kT"], st["vaug"], st["E"]
        blocks = [(s_, hd) for s_ in range(4) for hd in range(8)]

        def accpos(aset, idx):
            if idx < 3:
                return (4 if aset == 0 else 6), idx * 129
            return 5, (0 if aset == 0 else 129)

        def qk(i):
            s_, hd = blocks[i]
            pr = 2 * (i % 2)
            qTh = qT[:, hd, s_ * 256:(s_ + 1) * 256]
            for kc in range(2):
                kTa = kT[:, hd, s_ * 256 + kc * 128:s_ * 256 + (kc + 1) * 128]
                mm(psum[:, pr, kc * 256:(kc + 1) * 256], kTa[0:64, :], qTh[0:64, :], True, True, (("kT", s_), ("qT",)), (PK(pr),))
                mm(psum[:, pr + 1, kc * 256:(kc + 1) * 256], kTa[64:128, :], qTh[64:128, :], True, True, (("kT", s_), ("qT",)), (PK(pr + 1),))

        def ex(i):
            pr = 2 * (i % 2)
            Eb = E[i % 2]
            act(lambda h: h.activation(out=Eb[:], in_=psum[:, pr:pr + 2, :], func=AF.Exp, scale=0.125),
                (PK(pr), PK(pr + 1)), (("E", i % 2),))

        def pv(i):
            s_, hd = blocks[i]
            Eb = E[i % 2]
            for kc in range(2):
                va = vaug[:, s_ * 2 + kc, hd, :]
                for qt in range(2):
                    for m in range(2):
                        idx = qt * 2 + m
                        bank, col = accpos(i % 2, idx)
                        first = (kc == 0) and (idx in (0, 3))
                        mm(psum[:, bank, col:col + 129], Eb[:, m, kc * 256 + qt * 128:kc * 256 + (qt + 1) * 128], va, first, kc == 1,
                           (("E", i % 2), ("vaug", s_)), (PK(bank),), skip_group_check=True)

        def finalize(i):
            s_, hd = blocks[i]
            fin = st["fin"]
            b3, _ = accpos(i % 2, 0)
            _, c5 = accpos(i % 2, 3)
            dve(lambda h: h.tensor_copy(out=fin[:, 0:3, :], in_=psum[:, b3, 0:387].rearrange("p (a c) -> p a c", c=129)), (PK(b3),), ("fin",))
            dve(lambda h: h.tensor_copy(out=fin[:, 3, :], in_=psum[:, 5, c5:c5 + 129]), (PK(5),), ("fin",))
            st["ssb"] = ssbs[s_ % 2]
            st["ssbk"] = "ssb%d" % (s_ % 2)
            fin_math(l, st, 2, otok2[s_ % 2], 0, ("otok", s_ % 2), hd)

        qk(0)
        for i in range(32):
            if i + 1 < 32:
                qk(i + 1)
            ex(i)
            pv(i)
            finalize(i)
            s_, hd = blocks[i]
            if hd == 3 and s_ > 0:
                fin_seq(s_ - 1)
        fin_seq(3)

    def attn_post(l, st, otok, ntiles, okey):
        ssb = st["ssb"]
        sk_ = st.get("ssbk", "ssb")
        srk = sk_ + "_r"
        n = ntiles * 8
        rk = tuple((sk_, c) for c in range(n))
        dve(lambda h: h.tensor_scalar(out=ssb[:, 0:n], in0=ssb[:, 0:n], scalar1=1.0 / 128, scalar2=EPS, op0=ALU.mult, op1=ALU.add), rk, (srk,))
        act(lambda h: h.activation(out=ssb[:, 0:n], in_=ssb[:, 0:n], func=AF.Ln), (srk,), (srk,))
        act(lambda h: h.activation(out=ssb[:, 0:n], in_=ssb[:, 0:n], func=AF.Exp, scale=-0.5), (srk,), (srk,))
        for ti in range(ntiles):
            for hd in range(8):
                col = hd * ntiles + ti
                sl = otok[:, ti, hd * 128:(hd + 1) * 128]
                dve(lambda h, sl=sl, col=col: h.scalar_tensor_tensor(out=sl, in0=sl, scalar=ssb[:, col:col + 1], in1=subg[:, l, :],
                                                                     op0=ALU.mult, op1=ALU.mult), (srk, "subg", okey), (okey,) + rk)

    def otok_to_oT(otok, ntiles, oT, tile0, okey):
        for ti in range(ntiles):
            tt = tile0 + ti
            for half in range(2):
                psb = psum[:, 7, :].bitcast(BF16)[:, half * 512:(half + 1) * 512]
                for c4 in range(4):
                    c = half * 4 + c4
                    pe(lambda h, c4=c4, c=c, psb=psb, ti=ti: h.transpose(psb[:, c4 * 128:(c4 + 1) * 128], otok[:, ti, c * 128:(c + 1) * 128], identb[:]),
                       (okey, "identb"), (("psb", half),))
                dstv = oT[:, half * 4:(half + 1) * 4, tt * 128:(tt + 1) * 128]
                srcv = psb.rearrange("p (c t) -> p c t", c=4)
                if half == 0:
                    act(lambda h, dstv=dstv, srcv=srcv: h.activation(out=dstv, in_=srcv, func=AF.Copy), (("psb", half),), (("oT", tt // 4),))
                else:
                    dve(lambda h, dstv=dstv, srcv=srcv: h.tensor_copy(out=dstv, in_=srcv), (("psb", half),), (("oT", tt // 4),))

    def fourier_finish(AB, abkey, ABT, tt):
        for ab in range(2):
            psb = psum[:, 7, :].bitcast(BF16)[:, ab * 512:(ab + 1) * 512]
            for g in range(4):
                pe(lambda h, g=g, psb=psb, ab=ab: h.transpose(psb[:, g * 128:(g + 1) * 128], AB[:, ab, g * 128:(g + 1) * 128], identb[:]),
                   (abkey, "identb"), (("psb", ab),))
            t4 = tt % 4
            dstv = ABT[:, :, ab, t4 * 128:(t4 + 1) * 128]
            srcv = psb.rearrange("p (c t) -> p c t", c=4)
            if ab == 0:
                act(lambda h, dstv=dstv, srcv=srcv: h.activation(out=dstv, in_=srcv, func=AF.Copy), (("psb", ab),), ("ABT",))
            else:
                dve(lambda h, dstv=dstv, srcv=srcv: h.tensor_copy(out=dstv, in_=srcv), (("psb", ab),), ("ABT",))

    def chan_dft(ABT, frT, b):
        for g in range(4):
            bi = psbank(4)
            mm(psum[:, bi, :], dftc[:, 0, :], ABT[:, g, 0, :], True, False, ("dftc", "ABT"), (PK(bi),))
            mm(psum[:, bi, :], dftc[:, 1, :], ABT[:, g, 1, :], False, True, ("dftc", "ABT"), (PK(bi),))
            act(lambda h, g=g, b=b, bi=bi: h.activation(out=frT[:, g, b * 512:(b + 1) * 512], in_=psum[:, bi, :], func=AF.Copy),
                (PK(bi),), (("frT", b),))

    def stage_mix_out(grp, l, oT, frT, gsT, st):
        cond = grp["cond"]
        for pas in range(2):
            srcT, nk2, skey = (frT, 4, (lambda b: ("frT", b))) if pas == 0 else (oT, 8, (lambda b: ("oT", b)))
            for hh in range(2):
                wg, wgk = WQ_.next()
                wa, wak = WQ_.next()
                for c in range(4):
                    fo = hh * 4 + c
                    for b in range(2):
                        bg = psbank()
                        for k in range(8):
                            mm(psum[:, bg, :], wg[:, k, c * 128:(c + 1) * 128], hnT[:, k, b * 512:(b + 1) * 512], k == 0, k == 7,
                               (wgk, hk(b)), (PK(bg),))
                        ba = psbank()
                        for k in range(nk2):
                            mm(psum[:, ba, :], wa[:, k, c * 128:(c + 1) * 128], srcT[:, k, b * 512:(b + 1) * 512], k == 0, k == nk2 - 1,
                               (wak, skey(b)), (PK(ba),))
                        sg = st["sg"][(fo * 2 + b) % 2]
                        sgk = ("sg", (fo * 2 + b) % 2)
                        act(lambda h, sg=sg, bg=bg: h.activation(out=sg[:], in_=psum[:, bg, :], func=AF.Sigmoid), (PK(bg),), (sgk,))
                        gdst = gsT[:, fo, b * 512:(b + 1) * 512]
                        if pas == 0:
                            dve(lambda h, sg=sg, ba=ba, gdst=gdst: h.tensor_tensor(out=gdst, in0=sg[:], in1=psum[:, ba, :], op=ALU.mult),
                                (sgk, PK(ba)), (("gsT", b),))
                        else:
                            dve(lambda h, sg=sg, ba=ba: h.tensor_tensor(out=sg[:], in0=sg[:], in1=psum[:, ba, :], op=ALU.mult),
                                (sgk, PK(ba)), (sgk,))
                            dve(lambda h, sg=sg, gdst=gdst: h.tensor_tensor(out=gdst, in0=gdst, in1=sg[:], op=ALU.add),
                                (sgk, ("gsT", b)), (("gsT", b),))
        for hh in range(2):
            wt, wk = WQ_.next()

            def epi_o(fo, b, bi):
                T0 = grp["tok0"] + b * 512
                dve(lambda h: h.scalar_tensor_tensor(out=xT[:, fo, T0:T0 + 512], in0=psum[:, bi, :],
                                                     scalar=mods[:, l, 2 * 8 + fo, cond:cond + 1], in1=xT[:, fo, T0:T0 + 512],
                                                     op0=ALU.mult, op1=ALU.add), (PK(bi), "mods", ("xT", T0 // 512)), (("xT", T0 // 512),))
            lin_fm(wt, wk, 8, gsT, lambda b: ("gsT", b), 4, epi_o, chunk0=hh * 4)

    def stage_ffn(grp, l, st):
        cond = grp["cond"]
        nseq, L = grp["nseq"], grp["L"]
        hT = st["hT"]
        halo = None
        if grp["rope"]:
            hst = st["hst"]
            dve(lambda h: h.tensor_copy(out=hst[:, :, 0:1], in_=hnT[:, :, 0:1]), (hk(0),), ("hst",))
            dve(lambda h: h.tensor_copy(out=hst[:, :, 1:2], in_=hnT[:, :, 1023:1024]), (hk(1),), ("hst",))
            P.dma("sp", ib_h[l][:, 0:16], hst[:].rearrange("p k w -> p (k w)"), reads=("hst",), writes=(("ibh", l),))
            agather(ib_h[l], ob_h[l], [("ibh", l)], ("obh", l))
            hall = st["hall"]
            P.dma("sp", hall[:], ob_h[l].rearrange("(r p) n -> p r n", p=128)[:, :, 0:16], reads=(("obh", l),), writes=("hall",))
            hsel32 = st["hsel32"]
            hv = hall[:].rearrange("p r (k w) -> p r k w", w=2)
            for w_, so in ((0, 0), (1, 4)):
                srcw = 1 - w_
                for r in range(4):
                    sc = pfm[:, O_SEL + so + r:O_SEL + so + r + 1]
                    if r == 0:
                        dve(lambda h, sc=sc, r=r, srcw=srcw, w_=w_: h.tensor_scalar(out=hsel32[:, :, w_], in0=hv[:, r, :, srcw], scalar1=sc, scalar2=None, op0=ALU.mult),
                            ("hall", "pfm"), ("hsel32",))
                    else:
                        dve(lambda h, sc=sc, r=r, srcw=srcw, w_=w_: h.scalar_tensor_tensor(out=hsel32[:, :, w_], in0=hv[:, r, :, srcw], scalar=sc, in1=hsel32[:, :, w_],
                                                                                          op0=ALU.mult, op1=ALU.add), ("hall", "pfm", "hsel32"), ("hsel32",))
            halo = st["hsel"]
            dve(lambda h: h.tensor_copy(out=halo[:], in_=hsel32[:]), ("hsel32",), ("hsel",))
        slot = 0
        for jj in range(11):
            wt, wk = WQ_.next()
            for j2 in range(2):
                j = jj * 2 + j2
                cbuf = []
                for vg in range(2):
                    pb0 = 2 * (slot % 3)
                    slot += 1
                    f = vg * 22 + j
                    wofs = O_CW + l * 132
                    w0 = pfm[:, wofs + f:wofs + f + 1]
                    w1 = pfm[:, wofs + 44 + f:wofs + 44 + f + 1]
                    w2 = pfm[:, wofs + 88 + f:wofs + 88 + f + 1]
                    bb = pfm[:, O_CB + l * 44 + f:O_CB + l * 44 + f + 1]
                    wsl = lambda k, vg=vg, j2=j2: wt[:, k, vg, j2 * 128:(j2 + 1) * 128]
                    for b in range(2):
                        for k in range(8):
                            mm(psum[:, pb0 + b, :], wsl(k), hnT[:, k, b * 512:(b + 1) * 512], k == 0, k == 7,
                               (wk, hk(b)), (PK(pb0 + b),))
                    hb = 6 + (slot % 2)
                    if halo is not None:
                        for k in range(8):
                            mm(psum[:, hb, 0:2], wsl(k), halo[:, k, :], k == 0, k == 7, (wk, "hsel"), (PK(hb),))
                    cb = st["cbuf"][vg]
                    ckey = ("cbuf", vg)
                    pk = (PK(pb0), PK(pb0 + 1))
                    pfull = psum[:, pb0:pb0 + 2, :].rearrange("p a n -> p (a n)")
                    act(lambda h, cb=cb, pfull=pfull, w1=w1, bb=bb: h.activation(out=cb[:], in_=pfull, func=AF.Identity, scale=w1, bias=bb),
                        pk + ("pfm",), (ckey,))
                    cv = cb[:].rearrange("p (s t) -> p s t", s=nseq)
                    pv = pfull.rearrange("p (s t) -> p s t", s=nseq)
                    dve(lambda h, cv=cv, pv=pv, w0=w0: h.scalar_tensor_tensor(out=cv[:, :, 1:L], in0=pv[:, :, 0:L - 1], scalar=w0, in1=cv[:, :, 1:L],
                                                                             op0=ALU.mult, op1=ALU.add), pk + ("pfm", ckey), (ckey,))
                    dve(lambda h, cv=cv, pv=pv, w2=w2: h.scalar_tensor_tensor(out=cv[:, :, 0:L - 1], in0=pv[:, :, 1:L], scalar=w2, in1=cv[:, :, 0:L - 1],
                                                                             op0=ALU.mult, op1=ALU.add), pk + ("pfm", ckey), (ckey,))
                    if halo is not None:
                        dve(lambda h, cb=cb, hb=hb, w0=w0: h.scalar_tensor_tensor(out=cb[:, 0:1], in0=psum[:, hb, 0:1], scalar=w0, in1=cb[:, 0:1],
                                                                                 op0=ALU.mult, op1=ALU.add), (PK(hb), "pfm", ckey), (ckey,))
                        dve(lambda h, cb=cb, hb=hb, w2=w2: h.scalar_tensor_tensor(out=cb[:, 1023:1024], in0=psum[:, hb, 1:2], scalar=w2, in1=cb[:, 1023:1024],
                                                                                 op0=ALU.mult, op1=ALU.add), (PK(hb), "pfm", ckey), (ckey,))
                    cbuf.append((cb, ckey))
                cbv, cvk = cbuf[0]
                cbg, cgk = cbuf[1]
                act(lambda h, cbg=cbg: h.activation(out=cbg[:], in_=cbg[:], func=AF.Silu), (cgk,), (cgk,))
                dve(lambda h, cbv=cbv, cbg=cbg, j=j: h.tensor_tensor(out=hT[:, j, :], in0=cbg[:], in1=cbv[:], op=ALU.mult), (cvk, cgk), (("hT", j),))
        for fo in range(8):
            wt, wk = WQ_.next()
            for b in range(2):
                bi = psbank()
                for k in range(NJ):
                    mm(psum[:, bi, :], wt[:, k, :], hT[:, k, b * 512:(b + 1) * 512], k == 0, k == NJ - 1, (wk, ("hT", k)), (PK(bi),))
                T0 = grp["tok0"] + b * 512
                dve(lambda h, fo=fo, bi=bi, T0=T0: h.scalar_tensor_tensor(out=xT[:, fo, T0:T0 + 512], in0=psum[:, bi, :],
                                                                         scalar=mods[:, l, 5 * 8 + fo, cond:cond + 1], in1=xT[:, fo, T0:T0 + 512],
                                                                         op0=ALU.mult, op1=ALU.add), (PK(bi), "mods", ("xT", T0 // 512)), (("xT", T0 // 512),))

    def scope():
        ph = ExitStack()
        ph.callback(P.barrier)

        def alloc(name, shape, dt=F32):
            return ph.enter_context(_sbuf_tensor(name, list(shape), dt))
        return ph, alloc

    def do_norm(grp, l, which):
        ph, a = scope()
        with ph:
            scr = ([a("rstd0", [128, 512]), a("rstd1", [128, 512])], [a("sq0", [128, 8, 512], BF16), a("sq1", [128, 8, 512], BF16)],
                   [a("ntmp0", [128, 512]), a("ntmp1", [128, 512])])
            norm_mod(grp, l, which, hnT, scr)

    def attn_small(a, st, nq=512, ecols=None):
        st["E"] = [a(f"E{i}", [128, 2, ecols or nq], BF16) for i in range(2)]
        st["rec"] = a("rec", [128, 8])
        st["o32"] = a("o32", [128, nq // 128, 128])
        st["NT"] = 8 if nq == 512 else 2
        st["fin"] = a("finbuf", [128, 2 * (nq // 128), 129])
        if "ssb" not in st:
            st["ssb"] = a("ssb", [128, 64 if nq == 512 else 16])

    nlayers = DEPTH if dbg is None else dbg.get("nlayers", DEPTH)
    stop = None if dbg is None else dbg.get("stop")

    def tokmix(grp, l, hook=None):
        ph0, a0 = scope()
        with ph0:
            frT = a0("frT", [128, 4, 1024], BF16)
            if grp["rope"]:
                qT = a0("qT", [128, 8, 1024], BF16)
            do_norm(grp, l, 0)
            if grp["rope"]:
                ph1, a1 = scope()
                with ph1:
                    st = dict()
                    st["stage_b"] = [a1("stgb0", [128, 512], BF16), a1("stgb1", [128, 512], BF16)]
                    st["ropeb"] = [a1("ropeb0", [128, 512], BF16), a1("ropeb1", [128, 512], BF16)]
                    st["ropet"] = [a1("ropet0", [128, 512]), a1("ropet1", [128, 512])]
                    st["kTs"] = a1("kTs", [128, 8, 1024], BF16)
                    st["rope"] = a1("rope_sb", [128, 2048])
                    P.dma("sp", st["rope"][:], rope_d[:, :], writes=("rope",))
                    if not (dbg or {}).get("nof"):
                        proj_f(grp, l, st)
                    if (dbg or {}).get("extraw"):
                        WQ_.next()
                        WQ_.next()
                        WQ_.next()
                    if stop != "projf":
                        proj_qkv(grp, l, st, qT)
                if stop in ("projf", "proj", "projq"):
                    return
                if hook is not None:
                    hook()
                ph1, a1 = scope()
                with ph1:
                    fbuf = a1("fbuf", [128, 32, 512], BF16)
                    cs = a1("cs0", [128, 2, 32, 128], BF16)
                    AB = [a1("AB0", [128, 2, 512], BF16)]
                    ABT = a1("ABT", [128, 4, 2, 512], BF16)
                    for q4 in range(4):
                        P.dma("sp", fbuf[:, q4 * 8:(q4 + 1) * 8, :], ob_f[l].rearrange("(c p) n -> p c n", p=128)[:, q4 * 8:(q4 + 1) * 8, :],
                              reads=(("obf", l),), writes=(("fbuf", q4),))
                    for tt in range(8):
                        P.dma("sp", cs[:], dft4k_d[tt], writes=("cs",))
                        ba, bb_ = psbank(4), psbank(4)
                        for ci, bnk in ((0, ba), (1, bb_)):
                            for kc in range(32):
                                mm(psum[:, bnk, :], cs[:, ci, kc, :], fbuf[:, kc, :], kc == 0, kc == 31, ("cs", ("fbuf", kc // 8)), (PK(bnk),))
                        ab = AB[0]
                        act(lambda h, ab=ab, ba=ba: h.activation(out=ab[:, 0, :], in_=psum[:, ba, :], func=AF.Copy), (PK(ba),), ("AB",))
                        dve(lambda h, ab=ab, bb_=bb_: h.tensor_copy(out=ab[:, 1, :], in_=psum[:, bb_, :]), (PK(bb_),), ("AB",))
                        fourier_finish(ab, "AB", ABT, tt)
                        if tt % 4 == 3:
                            chan_dft(ABT, frT, tt // 4)
                if stop == "fourier":
                    return
                ph1, a1 = scope()
                with ph1:
                    otok = a1("otok", [128, 8, 1024], BF16)
                    st = dict()
                    st["ssb"] = a1("ssb", [128, 64])
                    ph2, a2 = scope()
                    with ph2:
                        attn_small(a2, st)
                        kbs = [a2("kbuf0", [128, 4352], BF16), a2("kbuf1", [128, 4352], BF16)]
                        vbs = [a2("vbuf0", [128, 34, 129], BF16), a2("vbuf1", [128, 34, 129], BF16)]
                        kstg = a2("kstg", [128, 2, 1024], BF16)
                        for i_ in range(2):
                            dve(lambda h, i_=i_: h.memset(vbs[i_][:, :, 128:129], 1.0), (), (("vbuf", i_, "o"),))
                        P.dma("pool", kstg[:], cache_k[l].rearrange("(c p) n -> p c n", p=128), writes=("kstg",))

                        def load_kv(hd):
                            kb, vb, pi = kbs[hd % 2], vbs[hd % 2], hd % 2
                            psb = psum[:, 7, :].bitcast(BF16)[:, 0:256]
                            for c in range(2):
                                pe(lambda h, c=c, hd=hd, psb=psb: h.transpose(psb[:, c * 128:(c + 1) * 128], kstg[:, c, hd * 128:(hd + 1) * 128], identb[:]),
                                   ("kstg", "identb"), (("psb", 0),))
                            dve(lambda h, psb=psb, kb=kb: h.tensor_copy(out=kb[:, 0:256], in_=psb), (("psb", 0),), (("kbuf", pi, "c"),))
                            t, hq = hd // 4, hd % 4
                            P.dma("sp", kb[:, 256:4352].rearrange("p (r n) -> p r n", r=4),
                                  ob_k[l][t].rearrange("(r f) n -> f r n", r=4)[hq * 128:(hq + 1) * 128, :, :],
                                  reads=(("obk", l, t),), writes=(("kbuf", pi, "g"),))
                            P.dma("pool", vb[:, 0:2, 0:128], cache_v[l].rearrange("(c p) n -> p c n", p=128)[:, :, hd * 128:(hd + 1) * 128],
                                  writes=(("vbuf", pi, "c"),))
                            P.dma("sp", vb[:, 2:34, 0:128], ob_v[l][t].rearrange("(c p) n -> p c n", p=128)[:, :, hq * 128:(hq + 1) * 128],
                                  reads=(("obv", l, t),), writes=(("vbuf", pi, "g"),))

                        load_kv(0)
                        for hd in range(8):
                            if hd + 1 < 8:
                                load_kv(hd + 1)
                            kb, vb, pi = kbs[hd % 2], vbs[hd % 2], hd % 2
                            kvk = (("kbuf", pi, "c"), ("kbuf", pi, "g"), ("vbuf", pi, "c"), ("vbuf", pi, "g"), ("vbuf", pi, "o"))
                            for qb in range(2):
                                attn_block(l, qT[:, hd, qb * 512:(qb + 1) * 512], 512, 34,
                                           lambda kc, kb=kb: kb[:, kc * 128:(kc + 1) * 128],
                                           lambda kc, vb=vb: vb[:, kc, :],
                                           kvk, st, otok, qb * 4, ("otok", 0), hd)
                    oT = a1("oT", [128, 8, 1024], BF16)
                    attn_post(l, st, otok, 8, ("otok", 0))
                    otok_to_oT(otok, 8, oT, 0, ("otok", 0))
                    if stop == "attn":
                        return
                    ph2, a2 = scope()
                    with ph2:
                        gsT = a2("gsT", [128, 8, 1024], BF16)
                        st = dict(sg=[a2("sg0", [128, 512]), a2("sg1", [128, 512])])
                        stage_mix_out(grp, l, oT, frT, gsT, st)
            else:
                ph1, a1 = scope()
                with ph1:
                    oT = a1("oT", [128, 8, 1024], BF16)
                    ph2, a2 = scope()
                    with ph2:
                        st = dict()
                        qT = a2("qT", [128, 8, 1024], BF16)
                        st["kT"] = a2("kT", [128, 8, 1024], BF16)
                        st["vaug"] = a2("vaug", [128, 8, 8, 129], BF16)
                        otok2 = [a2("otok0", [128, 2, 1024], BF16), a2("otok1", [128, 2, 1024], BF16)]
                        vaug = st["vaug"]
                        dve(lambda h: h.memset(vaug[:, :, :, 128:129], 1.0), (), (("vaug", 0), ("vaug", 1), ("vaug", 2), ("vaug", 3)))
                        ph3, a3 = scope()
                        with ph3:
                            st["stage_f"] = [a3("stgf0", [128, 512]), a3("stgf1", [128, 512])]
                            st["stage_b"] = [a3("stgb0", [128, 512], BF16), a3("stgb1", [128, 512], BF16)]
                            proj_qkv(grp, l, st, qT)
                        attn_small(a2, st, nq=256, ecols=512)
                        ssbs = [st["ssb"], a2("ssb_b", [128, 16])]

                        def fin(sq):
                            st["ssb"] = ssbs[sq % 2]
                            st["ssbk"] = "ssb%d" % (sq % 2)
                            attn_post(l, st, otok2[sq % 2], 2, ("otok", sq % 2))
                            otok_to_oT(otok2[sq % 2], 2, oT, sq * 2, ("otok", sq % 2))
                        attn_prompt(l, st, qT, otok2, ssbs, fin)
                    ph2, a2 = scope()
                    with ph2:
                        ph3, a3 = scope()
                        with ph3:
                            st = dict()
                            st["ftok"] = a3("ftok", [128, 8, 512], BF16)
                            AB = [a3("AB0", [128, 2, 512], BF16), a3("AB1", [128, 2, 512], BF16)]
                            ABT = a3("ABT", [128, 4, 2, 512], BF16)
                            proj_f(grp, l, st)
                            for s_ in range(4):
                                for jt in range(2):
                                    tt = s_ * 2 + jt
                                    ba, bb_ = psbank(4), psbank(4)
                                    for csi, bnk in ((0, ba), (1, bb_)):
                                        for kc in range(2):
                                            mm(psum[:, bnk, :], dft256[:, csi, kc, jt * 128:(jt + 1) * 128], st["ftok"][:, s_ * 2 + kc, :], kc == 0, kc == 1,
                                               ("dft256", ("ftok", s_)), (PK(bnk),))
                                    ab = AB[tt % 2]
                                    act(lambda h, ab=ab, ba=ba: h.activation(out=ab[:, 0, :], in_=psum[:, ba, :], func=AF.Copy), (PK(ba),), (("AB", tt % 2),))
                                    dve(lambda h, ab=ab, bb_=bb_: h.tensor_copy(out=ab[:, 1, :], in_=psum[:, bb_, :]), (PK(bb_),), (("AB", tt % 2),))
                                    fourier_finish(ab, ("AB", tt % 2), ABT, tt)
                                    if tt % 4 == 3:
                                        chan_dft(ABT, frT, tt // 4)
                        gsT = a2("gsT", [128, 8, 1024], BF16)
                        st = dict(sg=[a2("sg0", [128, 512]), a2("sg1", [128, 512])])
                        stage_mix_out(grp, l, oT, frT, gsT, st)

    def ffn(grp, l):
        do_norm(grp, l, 1)
        ph0, a0 = scope()
        with ph0:
            st = dict()
            st["hT"] = a0("hT", [128, NJ, 1024], BF16)
            st["cbuf"] = [a0("cbufv", [128, 1024]), a0("cbufg", [128, 1024])]
            st["uh"] = [a0("uh0", [128, 2]), a0("uh1", [128, 2])]
            if grp["rope"]:
                st["hst"] = a0("hst", [128, 8, 2], BF16)
                st["hall"] = a0("hall", [128, 4, 16], BF16)
                st["hsel32"] = a0("hsel32", [128, 8, 2])
                st["hsel"] = a0("hsel", [128, 8, 2], BF16)
            stage_ffn(grp, l, st)


    for l in range(nlayers):
        def hook(l=l):
            if l > 0:
                ffn(GP, l - 1)
            if l + 1 < nlayers:
                compute_mods(l + 1)
            if l > 0:
                do_norm(GS, l, 0)
        tokmix(GS, l, hook)
        if stop is not None:
            continue
        ffn(GS, l)
        tokmix(GP, l)
    if stop is None and nlayers > 0:
        ffn(GP, nlayers - 1)

    P.barrier()
    with ExitStack() as ph:
        def p2(name, shape, dt=F32, ph=ph):
            return ph.enter_context(_sbuf_tensor(name, list(shape), dt))
        rstd = p2("rstd", [128, 512])
        sq = p2("sq", [128, 8, 512], BF16)
        ynT = p2("ynT", [128, 8, 512])
        ytok = [p2("ytok0", [128, D]), p2("ytok1", [128, D])]
        for blk in range(4):
            T0 = blk * 512
            rstd_block(T0, rstd, sq)
            for c in range(8):
                dve(lambda h, c=c: h.tensor_tensor(out=ynT[:, c, :], in0=xT[:, c, T0:T0 + 512], in1=rstd[:], op=ALU.mult),
                    (("xT", blk), ("rstd", 0)), ("ynT",))
                act(lambda h, c=c: h.activation(out=ynT[:, c, :], in_=ynT[:, c, :], func=AF.Identity, scale=pfm[:, O_FG + c:O_FG + c + 1]),
                    ("ynT", "pfm"), ("ynT",))
            for t4 in range(4):
                tt = blk * 4 + t4
                yb = ytok[tt % 2]
                for half in range(2):
                    bi = psbank()
                    for c4 in range(4):
                        c = half * 4 + c4
                        pe(lambda h, bi=bi, c4=c4, c=c, t4=t4: h.transpose(psum[:, bi, c4 * 128:(c4 + 1) * 128], ynT[:, c, t4 * 128:(t4 + 1) * 128], identf[:]),
                           ("ynT", "identf"), (PK(bi),))
                    if half == 0:
                        act(lambda h, yb=yb, bi=bi: h.activation(out=yb[:, 0:512], in_=psum[:, bi, :], func=AF.Copy), (PK(bi),), (("ytok", tt % 2),))
                    else:
                        dve(lambda h, yb=yb, bi=bi: h.tensor_copy(out=yb[:, 512:1024], in_=psum[:, bi, :]), (PK(bi),), (("ytok", tt % 2),))
                P.dma("sp", y_d[tt * 128:(tt + 1) * 128, :], yb[:], reads=(("ytok", tt % 2),), writes=(okey_new("y"),))
    P.finish_waits("sp", tuple(out_keys))
    P.emit()
    es.close()
    return nc


def _fm(a, nchunk):
    a = np.asarray(a, np.float32)
    lead = a.shape[:-1]
    a = a.reshape(*lead, nchunk, 128)
    a = np.moveaxis(a, -1, 0)
    return np.ascontiguousarray(a)


def _consts():
    bf = ml_dtypes.bfloat16
    identf = np.eye(128, dtype=np.float32)
    identb = np.eye(128).astype(bf)
    sw = np.zeros((128, 128), np.float32)
    for p in range(128):
        m, d = p // 64, p % 64
        base = 32 if d >= 32 else 0
        dd = d - base
        partner = m * 64 + base + (dd + 16 if dd < 16 else dd - 16)
        sw[partner, p] = 1.0
    c = np.arange(128)
    ang = 2 * np.pi * ((c[:, None] * c[None, :]) % 128) / 128
    dftc = np.stack([np.cos(ang), -np.sin(ang)], 1) / np.sqrt(128.0)
    t = np.arange(256)
    a256 = 2 * np.pi * ((t[:, None] * t[None, :]) % 256) / 256
    C = np.cos(a256) / 16.0
    S = np.sin(a256) / 16.0
    d256 = np.stack([C.reshape(2, 128, 256), S.reshape(2, 128, 256)], 0)
    d256 = np.transpose(d256, (2, 0, 1, 3))
    return dict(identf=identf, identb=identb, swapp=sw.astype(bf), dftc=dftc.astype(bf), dft256=np.ascontiguousarray(d256).astype(bf))


def _dft4k(r):
    bf = ml_dtypes.bfloat16
    t = np.arange(4096, dtype=np.int64)
    tp = r * 1024 + np.arange(1024, dtype=np.int64)
    ang = 2 * np.pi * ((t[:, None] * tp[None, :]) % 4096) / 4096.0
    C = (np.cos(ang) / 64.0).astype(np.float32)
    S = (np.sin(ang) / 64.0).astype(np.float32)
    a = np.stack([C, S], 0).reshape(2, 32, 128, 8, 128)
    a = np.transpose(a, (3, 2, 0, 1, 4))
    return np.ascontiguousarray(a).astype(bf)


def _rope_tables(r):
    pos = r * 1024 + np.arange(1024)
    row = (pos // 64).astype(np.float32)
    col = (pos % 64).astype(np.float32)
    inv = (1.0 / (10000.0 ** (np.arange(0, 32, 2, dtype=np.float32) / 32))).astype(np.float32)
    cos = np.zeros((128, 1024), np.float32)
    sin = np.zeros((128, 1024), np.float32)
    for p in range(128):
        d = p % 64
        if d < 32:
            a = row * inv[d % 16]
            dd = d
        else:
            a = col * inv[(d - 32) % 16]
            dd = d - 32
        a = a.astype(np.float32)
        cos[p] = np.cos(a)
        sin[p] = np.sin(a) * (-1.0 if dd < 16 else 1.0)
    return cos, sin


_NC_CACHE = {}


def kernel(x_prompt, x_sample, c, cache_k, cache_v, c_ctx, norm1_g, norm2_g, final_g,
           w_ada, b_ada, w_in, w_fourier, lam_params, subln_g, w_attn, w_o,
           w_up, conv_w, conv_b, w_down, _dbg=None):
    f32 = np.float32
    A = lambda a: np.ascontiguousarray(np.asarray(a, dtype=f32))
    x_prompt, x_sample, c, cache_k, cache_v, c_ctx = map(A, (x_prompt, x_sample, c, cache_k, cache_v, c_ctx))
    w_ada, w_in, w_fourier, w_attn, w_o, w_up, w_down = map(A, (w_ada, w_in, w_fourier, w_attn, w_o, w_up, w_down))
    consts = _consts()
    pbc = np.concatenate([A(lam_params).reshape(-1), A(subln_g).reshape(-1)])[None, :].repeat(128, 0)
    pbc = np.ascontiguousarray(pbc, dtype=f32)
    in_maps = []
    for core in range(8):
        g, r = core // 4, core % 4
        xin = np.concatenate([x_sample[g, r * 1024:(r + 1) * 1024], x_prompt[4 * core:4 * core + 4].reshape(1024, D)], 0)
        pfm = np.zeros((128, NPF), f32)
        cond = np.stack([c[g], c_ctx], 0)
        pfm[:, O_COND:O_COND + 16] = np.transpose(_fm(cond, 8), (0, 2, 1)).reshape(128, 16)
        pfm[:, O_N1G:O_N1G + 32] = _fm(norm1_g, 8).reshape(128, 32)
        pfm[:, O_N2G:O_N2G + 32] = _fm(norm2_g, 8).reshape(128, 32)
        pfm[:, O_FG:O_FG + 8] = _fm(final_g, 8).reshape(128, 8)
        pfm[:, O_BADA:O_BADA + 192] = _fm(b_ada, 48).reshape(128, 192)
        pfm[:, O_CW:O_CW + 528] = _fm(conv_w, 44).reshape(128, 528)
        pfm[:, O_CB:O_CB + 176] = _fm(conv_b, 44).reshape(128, 176)
        cos, sin = _rope_tables(r)
        rope = np.ascontiguousarray(np.concatenate([cos, sin], 1))
        sel = np.zeros(8, f32)
        if r > 0:
            sel[r - 1] = 1.0
        if r < 3:
            sel[4 + r + 1] = 1.0
        pfm[:, O_SEL:O_SEL + 8] = sel[None, :]
        m = dict(xin=np.ascontiguousarray(xin), w_ada=w_ada, w_in=w_in, w_fourier=w_fourier, w_attn=w_attn, w_o=w_o,
                 w_up=w_up, w_down=w_down,
                 cache_k=np.ascontiguousarray(cache_k[g].reshape(DEPTH, 256, D)),
                 cache_v=np.ascontiguousarray(cache_v[g].reshape(DEPTH, 256, D)),
                 pfm=pfm, pbc=pbc, rope=rope, dft4k=_dft4k(r), **consts)
        in_maps.append(m)
    if _dbg is not None:
        wl = max(_dbg.get("nlayers", DEPTH), 1)
        for m in in_maps:
            for nm in ("w_ada", "w_in", "w_fourier", "w_attn", "w_o", "w_up", "w_down"):
                m[nm] = np.ascontiguousarray(m[nm][:wl])
    key = repr(_dbg)
    if key not in _NC_CACHE:
        _NC_CACHE[key] = build_program(_dbg)
    nc = _NC_CACHE[key]
    res = run_bass_kernel_spmd(nc, in_maps, core_ids=list(range(8)))
    y_prompt = np.zeros((32, 256, D), f32)
    y_sample = np.zeros((2, 4096, D), f32)
    state_k = np.zeros((32, DEPTH, 256, 8, 2, 64), f32)
    state_v = np.zeros((32, DEPTH, 256, 8, 128), f32)
    for core in range(8):
        g, r = core // 4, core % 4
        o = res.results[core]
        y = np.asarray(o["y"], f32)
        y_sample[g, r * 1024:(r + 1) * 1024] = y[0:1024]
        y_prompt[4 * core:4 * core + 4] = y[1024:2048].reshape(4, 256, D)
        state_k[4 * core:4 * core + 4] = np.asarray(o["sk"], f32).reshape(4, DEPTH, 256, 8, 2, 64)
        state_v[4 * core:4 * core + 4] = np.asarray(o["sv"], f32).reshape(4, DEPTH, 256, 8, 128)
    return (y_prompt, y_sample, state_k, state_v)
```
